# Optimizing a Trainium2 kernel written in Bass

```python
import math
import jax, jax.numpy as jnp
from jax import lax
import numpy as np

D_MODEL = 2048
BATCH = 2
SEQ = 4096
DEPTH = 4

N_MIXERS = 2
N_HEADS = 16
HEAD_DIM = D_MODEL // N_HEADS
D_FF = 5632
CONV_WIDTH = 31
MOBA_BLOCK = 256
MOBA_TOPK = 3
Q_CHUNK = 64
NUM_BUCKETS = 32
MAX_DISTANCE = 2048
NORM_EPS = 1e-6
NEG_INF = -1e30
N_CONV_LAYERS = (DEPTH + 1) // 2
N_ATTN_LAYERS = DEPTH // 2

kernel_name = "hybrid_conformer_conv_moba_macaron"


def rms_norm(x, g):
    xf = x.astype(jnp.float32)
    y = xf * lax.rsqrt(jnp.mean(xf * xf, axis=-1, keepdims=True) + NORM_EPS)
    return (y * g.astype(jnp.float32)).astype(x.dtype)


def layer_norm(x, g, b):
    xf = x.astype(jnp.float32)
    mu = jnp.mean(xf, axis=-1, keepdims=True)
    xc = xf - mu
    y = xc * lax.rsqrt(jnp.mean(xc * xc, axis=-1, keepdims=True) + NORM_EPS)
    return (y * g.astype(jnp.float32) + b.astype(jnp.float32)).astype(x.dtype)


def swiglu_ffn(h, w_gate, w_up, w_down):
    return (jax.nn.silu(h @ w_gate) * (h @ w_up)) @ w_down


def conformer_conv(h, pw1_w, pw1_b, dw_w, dw_b, ln_g, ln_b, pw2_w, pw2_b):
    u = h @ pw1_w + pw1_b
    a, g = jnp.split(u, 2, axis=-1)
    u = a * jax.nn.sigmoid(g)
    u = lax.conv_general_dilated(
        u, dw_w[:, None, :].astype(u.dtype), window_strides=(1,),
        padding=[(CONV_WIDTH - 1, 0)],
        dimension_numbers=("NWC", "WIO", "NWC"),
        feature_group_count=D_MODEL) + dw_b
    u = jax.nn.silu(layer_norm(u, ln_g, ln_b))
    return u @ pw2_w + pw2_b


def t5_bucket(dist):
    max_exact = NUM_BUCKETS // 2
    n = jnp.maximum(dist, 0)
    nf = jnp.maximum(n, 1).astype(jnp.float32)
    large = max_exact + (jnp.log(nf / max_exact) / math.log(MAX_DISTANCE / max_exact)
                         * (NUM_BUCKETS - max_exact)).astype(jnp.int32)
    large = jnp.minimum(large, NUM_BUCKETS - 1)
    return jnp.where(n < max_exact, n, large)


_gather_blocks = jax.vmap(jax.vmap(lambda blocks, idx: blocks[idx]))


def moba_attention(h, wqkv, q_norm, k_norm, wo, rel_bias):
    B, S, _ = h.shape
    nb = -(-S // MOBA_BLOCK)
    s_pad = nb * MOBA_BLOCK
    f32 = jnp.float32
    qkv = (h @ wqkv).reshape(B, S, 3, N_HEADS, HEAD_DIM)
    q = rms_norm(qkv[:, :, 0], q_norm)
    k = rms_norm(qkv[:, :, 1], k_norm)
    v = qkv[:, :, 2]

    def to_bhsd(t):
        return jnp.pad(t.transpose(0, 2, 1, 3), ((0, 0), (0, 0), (0, s_pad - S), (0, 0)))

    q, k, v = to_bhsd(q), to_bhsd(k), to_bhsd(v)
    kb = k.reshape(B, N_HEADS, nb, MOBA_BLOCK, HEAD_DIM)
    vb = v.reshape(B, N_HEADS, nb, MOBA_BLOCK, HEAD_DIM)

    k_mean = jnp.mean(kb.astype(f32), axis=3)
    gate = jnp.einsum("bhsd,bhnd->bhsn", q.astype(f32), k_mean)
    pos = jnp.arange(s_pad)
    own_blk = pos // MOBA_BLOCK
    past = jnp.arange(nb)[None, :] < own_blk[:, None]
    gate = jnp.where(past, gate, -jnp.inf)
    k_sel = min(MOBA_TOPK, nb)
    _, sel = lax.top_k(gate, k_sel)
    sel_valid = jnp.arange(k_sel)[None, :] < own_blk[:, None]

    scale = HEAD_DIM ** -0.5
    bias_table = rel_bias.T.astype(f32)
    head_idx = jnp.arange(N_HEADS)[None, :, None, None]
    offs = jnp.arange(MOBA_BLOCK)

    def chunk(c):
        q0 = c * Q_CHUNK
        qc = lax.dynamic_slice_in_dim(q, q0, Q_CHUNK, axis=2)
        q_pos = q0 + jnp.arange(Q_CHUNK)
        blk = q0 // MOBA_BLOCK
        sel_c = lax.dynamic_slice_in_dim(sel, q0, Q_CHUNK, axis=2)
        valid_c = lax.dynamic_slice_in_dim(sel_valid, q0, Q_CHUNK, axis=0)
        logits = []
        for r in range(k_sel):
            idx = sel_c[..., r]
            k_r = _gather_blocks(kb, idx)
            s_r = jnp.einsum("bhqd,bhqkd->bhqk", qc, k_r, preferred_element_type=f32)
            k_pos = idx[..., None] * MOBA_BLOCK + offs
            bias = bias_table[head_idx, t5_bucket(q_pos[None, None, :, None] - k_pos)]
            logits.append(jnp.where(valid_c[None, None, :, r, None], s_r * scale + bias, NEG_INF))
        k_own = lax.dynamic_slice_in_dim(kb, blk, 1, axis=2)[:, :, 0]
        v_own = lax.dynamic_slice_in_dim(vb, blk, 1, axis=2)[:, :, 0]
        s_o = jnp.einsum("bhqd,bhkd->bhqk", qc, k_own, preferred_element_type=f32)
        dist = q_pos[:, None] - (blk * MOBA_BLOCK + offs)[None, :]
        bias_o = bias_table[:, t5_bucket(dist)]
        logits.append(jnp.where(dist >= 0, s_o * scale + bias_o[None], NEG_INF))
        probs = jax.nn.softmax(jnp.concatenate(logits, axis=-1), axis=-1)
        p_parts = jnp.split(probs.astype(v.dtype), k_sel + 1, axis=-1)
        out = jnp.einsum("bhqk,bhkd->bhqd", p_parts[-1], v_own, preferred_element_type=f32)
        for r in range(k_sel):
            v_r = _gather_blocks(vb, sel_c[..., r])
            out = out + jnp.einsum("bhqk,bhqkd->bhqd", p_parts[r], v_r, preferred_element_type=f32)
        return out.astype(h.dtype)

    n_chunks = s_pad // Q_CHUNK
    out = lax.map(chunk, jnp.arange(n_chunks))
    out = out.transpose(1, 0, 3, 2, 4).reshape(B, s_pad, D_MODEL)[:, :S]
    return out @ wo


def _dense(key, shape, fan_in):
    return jax.random.normal(key, shape, jnp.float32) * (fan_in ** -0.5)


def _gain(key, shape):
    return 1.0 + 0.01 * jax.random.normal(key, shape, jnp.float32)


def _small(key, shape, s=0.01):
    return s * jax.random.normal(key, shape, jnp.float32)


def setup_inputs(seed: int = 0) -> dict:
    key = jax.random.key(seed)
    ks = jax.random.split(key, 24)
    L, NC, NA, D, F = DEPTH, N_CONV_LAYERS, N_ATTN_LAYERS, D_MODEL, D_FF
    return {
        "x": jax.random.normal(ks[0], (BATCH, SEQ, D), jnp.float32),
        "rel_bias": _small(ks[1], (NUM_BUCKETS, N_HEADS), 0.1),
        "ffn1_norm": _gain(ks[2], (L, D)),
        "ffn1_w_gate": _dense(ks[3], (L, D, F), D),
        "ffn1_w_up": _dense(ks[4], (L, D, F), D),
        "ffn1_w_down": _dense(ks[5], (L, F, D), F),
        "mix_norm": _gain(ks[6], (L, D)),
        "ffn2_norm": _gain(ks[7], (L, D)),
        "ffn2_w_gate": _dense(ks[8], (L, D, F), D),
        "ffn2_w_up": _dense(ks[9], (L, D, F), D),
        "ffn2_w_down": _dense(ks[10], (L, F, D), F),
        "conv_pw1_w": _dense(ks[11], (NC, D, 2 * D), D),
        "conv_pw1_b": _small(ks[12], (NC, 2 * D)),
        "conv_dw_w": _dense(ks[13], (NC, CONV_WIDTH, D), CONV_WIDTH),
        "conv_dw_b": _small(ks[14], (NC, D)),
        "conv_ln_g": _gain(ks[15], (NC, D)),
        "conv_ln_b": _small(ks[16], (NC, D)),
        "conv_pw2_w": _dense(ks[17], (NC, D, D), D),
        "conv_pw2_b": _small(ks[18], (NC, D)),
        "attn_wqkv": _dense(ks[19], (NA, D, 3 * D), D),
        "attn_q_norm": _gain(ks[20], (NA, HEAD_DIM)),
        "attn_k_norm": _gain(ks[21], (NA, HEAD_DIM)),
        "attn_wo": _dense(ks[22], (NA, D, D), D),
    }


def reference(x, rel_bias, ffn1_norm, ffn1_w_gate, ffn1_w_up, ffn1_w_down, mix_norm,
              ffn2_norm, ffn2_w_gate, ffn2_w_up, ffn2_w_down,
              conv_pw1_w, conv_pw1_b, conv_dw_w, conv_dw_b, conv_ln_g, conv_ln_b,
              conv_pw2_w, conv_pw2_b, attn_wqkv, attn_q_norm, attn_k_norm, attn_wo):
    for i in range(DEPTH):
        x = x + 0.5 * swiglu_ffn(rms_norm(x, ffn1_norm[i]), ffn1_w_gate[i], ffn1_w_up[i], ffn1_w_down[i])
        h = rms_norm(x, mix_norm[i])
        j = i // N_MIXERS
        if i % N_MIXERS == 0:
            x = x + conformer_conv(h, conv_pw1_w[j], conv_pw1_b[j], conv_dw_w[j], conv_dw_b[j],
                                   conv_ln_g[j], conv_ln_b[j], conv_pw2_w[j], conv_pw2_b[j])
        else:
            x = x + moba_attention(h, attn_wqkv[j], attn_q_norm[j], attn_k_norm[j], attn_wo[j], rel_bias)
        x = x + 0.5 * swiglu_ffn(rms_norm(x, ffn2_norm[i]), ffn2_w_gate[i], ffn2_w_up[i], ffn2_w_down[i])
    return x
```

```python
import numpy as np
from contextlib import ExitStack
import concourse.bass as bass
import concourse.mybir as mybir
from concourse.bass_utils import run_bass_kernel_spmd

F32 = mybir.dt.float32
BF16 = mybir.dt.bfloat16
ALU = mybir.AluOpType
AF = mybir.ActivationFunctionType
AX = mybir.AxisListType

NCORES = 8
CONV_W = 31
HALO = 32
BLK = 256
NBUCKET = 32
EPS = 1e-6
NEG = -1.0e4

FULL_CFG = dict(D=2048, F=5632, T=1024, S=4096, L=4)
DEBUG_STOP = None
RELAX_SAME_ENGINE = True
import os
ALL_READERS = os.environ.get('MK_ALLR', '0') == '1'


class _Stop(Exception):
    pass


def _dbg(tag):
    if DEBUG_STOP == tag:
        raise _Stop()


class Sched:
    ENGS = ("pe", "act", "dve", "pool", "sp")

    def __init__(self, nc, es):
        self.nc = nc
        self.es = es
        self.insts = []
        self.last_w = {}
        self.readers = {}
        self.dma_sems = {}
        self.arena_tags = {}
        self.eng_sems = {e: es.enter_context(nc.semaphore("sem_" + e)) for e in self.ENGS}

    def _dma_sem(self, key):
        if key not in self.dma_sems:
            self.dma_sems[key] = [self.es.enter_context(self.nc.semaphore("dsem_%d" % len(self.dma_sems))), 0]
        return self.dma_sems[key]

    def emit(self, eng, fn, reads=(), writes=(), dma=None, inc=16):
        i = len(self.insts)
        deps = set()
        reads = list(reads)
        for k in list(reads) + list(writes):
            a = self.arena_tags.get(k[0] if isinstance(k, tuple) else k)
            if a is not None and ("EPOCH", a) not in reads and ("EPOCH", a) not in writes:
                reads.append(("EPOCH", a))
        for r in reads:
            if r in self.last_w:
                deps.add(self.last_w[r])
        for w in writes:
            if w in self.last_w:
                deps.add(self.last_w[w])
            for rd in self.readers.get(w, {}).values():
                deps.add(rd)
        deps.discard(i)
        rec = dict(eng=eng, fn=fn, deps=deps, dma=dma, inc=inc, token=None)
        if dma is not None:
            s = self._dma_sem(dma)
            s[1] += inc
            rec["token"] = (s[0], s[1])
        self.insts.append(rec)
        rkey = eng if dma is None else ("dma", dma)
        if ALL_READERS:
            rkey = i
        for r in reads:
            self.readers.setdefault(r, {})[rkey] = i
        for w in writes:
            self.last_w[w] = i
            self.readers[w] = {}
        return i

    def fence(self, arena):
        fs = self.fence_scratch
        self.emit("dve", lambda e: e.memset(fs, 0.0), writes=[("EPOCH", arena), "FSCR"])

    def finalize(self, block):
        insts = self.insts
        needed = set()
        pos = {}
        npos = {e: 0 for e in self.ENGS}
        for j, rec in enumerate(insts):
            pos[j] = npos[rec["eng"]]
            npos[rec["eng"]] += 1
        for j, rec in enumerate(insts):
            keep = set()
            for i in rec["deps"]:
                src = insts[i]
                if src["dma"] is None and src["eng"] == rec["eng"] and rec["eng"] == "pe" and rec["dma"] is None:
                    continue
                if RELAX_SAME_ENGINE and src["dma"] is None and rec["dma"] is None and src["eng"] == rec["eng"] and pos[j] - pos[i] >= 2:
                    continue
                keep.add(i)
                if src["dma"] is None:
                    needed.add(i)
            rec["deps"] = keep
        cnt = {e: 0 for e in self.ENGS}
        for i, rec in enumerate(insts):
            if rec["dma"] is None and i in needed:
                cnt[rec["eng"]] += 1
                rec["token"] = (self.eng_sems[rec["eng"]], cnt[rec["eng"]])
                rec["signal"] = True
            else:
                rec["signal"] = rec["dma"] is not None
        progs = {e: [] for e in self.ENGS}
        waited = {e: {} for e in self.ENGS}
        for rec in insts:
            e = rec["eng"]
            need = {}
            for i in rec["deps"]:
                sem, val = insts[i]["token"]
                k = id(sem)
                if waited[e].get(k, 0) >= val:
                    continue
                if k not in need or need[k][1] < val:
                    need[k] = (sem, val)
            for k, (sem, val) in need.items():
                waited[e][k] = val
            progs[e].append((list(need.values()), rec))

        def runner(items):
            def run(eng):
                for waits, rec in items:
                    for sem, val in waits:
                        eng.wait_ge(sem, val)
                    ins = rec["fn"](eng)
                    if rec["signal"]:
                        if rec["dma"] is not None:
                            if rec["inc"] == 1:
                                ins.then_inc(rec["token"][0])
                            else:
                                ins.then_inc(rec["token"][0], rec["inc"])
                        else:
                            ins.then_inc(rec["token"][0], 1)
            return run

        fin = []
        for key, (sem, val) in self.dma_sems.items():
            fin.append((sem, val))
        sp_items = progs["sp"]

        def sp_run(eng):
            runner(sp_items)(eng)
            for e2 in self.ENGS:
                if cnt[e2] > 0:
                    eng.wait_ge(self.eng_sems[e2], cnt[e2])
            for sem, val in fin:
                eng.wait_ge(sem, val)

        block.tensor(runner(progs["pe"]))
        block.scalar(runner(progs["act"]))
        block.vector(runner(progs["dve"]))
        block.gpsimd(runner(progs["pool"]))
        block.sync(sp_run)


def lay_w(w):
    K, N = w.shape
    kc, oc = K // 128, N // 128
    t = w.reshape(kc, 128, oc, 128)
    t = np.ascontiguousarray(t.transpose(2, 1, 0, 3))
    return t.reshape(oc * 128, kc * 128)


def interleave_cols(a, b):
    K, N = a.shape
    oc = N // 128
    t = np.stack([a.reshape(K, oc, 128), b.reshape(K, oc, 128)], axis=2)
    return t.reshape(K, 2 * N)


def fm(v):
    return np.ascontiguousarray(v.reshape(-1, 128).T)


def t5_bucket_np(d):
    max_exact = NBUCKET // 2
    n = np.maximum(d, 0)
    nf = np.maximum(n, 1).astype(np.float32)
    large = max_exact + (np.log(nf / np.float32(max_exact)) / np.float32(np.log(2048 / max_exact))
                         * np.float32(NBUCKET - max_exact)).astype(np.int32)
    large = np.minimum(large, NBUCKET - 1)
    return np.where(n < max_exact, n, large)


class Cfg:
    def __init__(self, D, F, T, S, L):
        self.D, self.F, self.T, self.S, self.L = D, F, T, S, L
        self.DC = D // 128
        self.FC = F // 128
        self.FCH = self.FC // 2
        self.H = D // 128
        self.NB = S // T
        self.NG = NCORES // self.NB
        self.TN = min(512, T)
        self.NTH = T // self.TN
        self.NBLK = S // BLK
        self.NBC = T // BLK
        self.NQT = T // 128
        self.NKT = S // 128
        self.GW = S + T
        off = 0
        self.pv = {}

        def add(name, n):
            nonlocal off
            self.pv[name] = off
            off += n
        for l in range(L):
            add("n1_%d" % l, self.DC)
            add("nm_%d" % l, self.DC)
            add("n2_%d" % l, self.DC)
            if l % 2 == 0:
                add("pw1b_%d" % l, 2 * self.DC)
                add("dww_%d" % l, CONV_W * self.DC)
                add("dwb_%d" % l, self.DC)
                add("lng_%d" % l, self.DC)
                add("lnb_%d" % l, self.DC)
                add("pw2b_%d" % l, self.DC)
            else:
                add("qn_%d" % l, 1)
                add("kn_%d" % l, 1)
        self.NP = off
        self.cv_hasprev = 0
        self.cv_prevsel = 1
        self.cv_pastneg = 1 + self.NB
        self.cv_ownhot = self.cv_pastneg + self.NQT * self.NBLK
        self.NCV = self.cv_ownhot + self.NQT * self.NBLK

    def weight_specs(self):
        sp = []
        for l in range(self.L):
            sp.append(("wgu1_%d" % l, 2 * self.F, self.D))
            sp.append(("wd1_%d" % l, 2 * self.D, self.F // 2))
            if l % 2 == 0:
                sp.append(("wpw1_%d" % l, 2 * self.D, self.D))
                sp.append(("wpw2_%d" % l, self.D, self.D))
            else:
                sp.append(("wqkv_%d" % l, 3 * self.D, self.D))
                sp.append(("wo_%d" % l, self.D, self.D))
            sp.append(("wgu2_%d" % l, 2 * self.F, self.D))
            sp.append(("wd2_%d" % l, 2 * self.D, self.F // 2))
        return sp


def host_prepare(cfg, inputs):
    c = cfg
    g = {k: np.asarray(v, dtype=np.float32) for k, v in inputs.items()}
    shared = {}
    for l in range(c.L):
        for tag in ("1", "2"):
            wg, wu, wd = g["ffn%s_w_gate" % tag][l], g["ffn%s_w_up" % tag][l], g["ffn%s_w_down" % tag][l]
            shared["wgu%s_%d" % (tag, l)] = lay_w(interleave_cols(wg, wu))
            hf = c.F // 2
            shared["wd%s_%d" % (tag, l)] = np.concatenate([lay_w(wd[:hf]), lay_w(wd[hf:])], axis=0)
        j = l // 2
        if l % 2 == 0:
            w1 = g["conv_pw1_w"][j]
            shared["wpw1_%d" % l] = lay_w(interleave_cols(w1[:, :c.D], w1[:, c.D:]))
            shared["wpw2_%d" % l] = lay_w(g["conv_pw2_w"][j])
        else:
            wq = g["attn_wqkv"][j]
            K = wq.shape[0]
            t = np.stack([wq[:, i * c.D:(i + 1) * c.D].reshape(K, c.H, 128) for i in range(3)], axis=2)
            shared["wqkv_%d" % l] = lay_w(t.reshape(K, 3 * c.D))
            shared["wo_%d" % l] = lay_w(g["attn_wo"][j])
    pvec = np.zeros((128, c.NP), np.float32)

    def put(name, arr):
        o = c.pv[name]
        pvec[:, o:o + arr.shape[1]] = arr
    for l in range(c.L):
        put("n1_%d" % l, fm(g["ffn1_norm"][l]))
        put("nm_%d" % l, fm(g["mix_norm"][l]))
        put("n2_%d" % l, fm(g["ffn2_norm"][l]))
        j = l // 2
        if l % 2 == 0:
            b1 = g["conv_pw1_b"][j]
            ba, bg = fm(b1[:c.D]), fm(b1[c.D:])
            put("pw1b_%d" % l, np.stack([ba, bg], axis=2).reshape(128, 2 * c.DC))
            dw = g["conv_dw_w"][j]
            put("dww_%d" % l, np.ascontiguousarray(dw.reshape(CONV_W, c.DC, 128).transpose(2, 0, 1)).reshape(128, CONV_W * c.DC))
            put("dwb_%d" % l, fm(g["conv_dw_b"][j]))
            put("lng_%d" % l, fm(g["conv_ln_g"][j]))
            put("lnb_%d" % l, fm(g["conv_ln_b"][j]))
            put("pw2b_%d" % l, fm(g["conv_pw2_b"][j]))
        else:
            put("qn_%d" % l, g["attn_q_norm"][j].reshape(128, 1))
            put("kn_%d" % l, g["attn_k_norm"][j].reshape(128, 1))
    shared["pvec"] = pvec
    shared["relb"] = g["rel_bias"]
    shared["ident"] = np.eye(128, dtype=np.float32)
    es = np.zeros((c.NBLK, c.NBLK, 128), np.float32)
    for n in range(c.NBLK):
        es[n, n, :] = 1.0
    shared["esel"] = es.reshape(c.NBLK, c.NBLK * 128)
    x = g["x"]
    per_core = []
    for core in range(NCORES):
        b, cl = core // c.NB, core % c.NB
        d = {}
        d["xT"] = np.ascontiguousarray(x[b, cl * c.T:(cl + 1) * c.T, :].T)
        cv = np.zeros((128, c.NCV), np.float32)
        cv[:, c.cv_hasprev] = 1.0 if cl > 0 else 0.0
        if cl > 0:
            cv[:, c.cv_prevsel + cl - 1] = 1.0
        for qt in range(c.NQT):
            own = cl * c.NBC + (qt * 128) // BLK
            for n in range(c.NBLK):
                cv[:, c.cv_pastneg + qt * c.NBLK + n] = 0.0 if n < own else -3.0e38
                cv[:, c.cv_ownhot + qt * c.NBLK + n] = 1.0 if n == own else 0.0
        d["cvec"] = cv
        jj = np.arange(c.GW)
        dist = jj - (c.S - 1) + cl * c.T
        oh = np.zeros((33, c.GW), np.float32)
        bk = t5_bucket_np(dist.astype(np.int64))
        oh[bk[dist >= 0], jj[dist >= 0]] = 1.0
        oh[32, jj[dist < 0]] = 1.0
        d["oh"] = oh
        per_core.append(d)
    return shared, per_core


def build_program(cfg, ag_weights=True, n_layers=None, stop_after=None, steps=None):
    c = cfg
    L = c.L if n_layers is None else n_layers
    D, F, T, DC, FC, FCH, H, TN, NTH = c.D, c.F, c.T, c.DC, c.FC, c.FCH, c.H, c.TN, c.NTH
    nc = bass.Bass("TRN2", target_bir_lowering=False)
    es = ExitStack()
    S = Sched(nc, es)
    groups_all = [list(range(NCORES))]
    groups_b = [list(range(b * c.NB, (b + 1) * c.NB)) for b in range(c.NG)]

    xT_d = nc.dram_tensor("xT", [D, T], F32, kind="ExternalInput").ap()
    yT_d = nc.dram_tensor("yT", [D, T], F32, kind="ExternalOutput").ap()
    pvec_d = nc.dram_tensor("pvec", [128, c.NP], F32, kind="ExternalInput").ap()
    cvec_d = nc.dram_tensor("cvec", [128, c.NCV], F32, kind="ExternalInput").ap()
    relb_d = nc.dram_tensor("relb", [NBUCKET, H], F32, kind="ExternalInput").ap()
    ident_d = nc.dram_tensor("ident", [128, 128], F32, kind="ExternalInput").ap()
    esel_d = nc.dram_tensor("esel", [c.NBLK, c.NBLK * 128], F32, kind="ExternalInput").ap()
    oh_d = nc.dram_tensor("oh", [33, c.GW], F32, kind="ExternalInput").ap()
    unf = steps is not None
    if unf:
        ag_weights = False
        need = set()
        for kind, l in steps:
            if kind == "ffn1":
                need |= {"wgu1_%d" % l, "wd1_%d" % l}
            elif kind == "ffn2":
                need |= {"wgu2_%d" % l, "wd2_%d" % l}
            elif kind == "conv":
                need |= {"wpw1_%d" % l, "wpw2_%d" % l}
            elif kind == "attnA":
                need |= {"wqkv_%d" % l}
            elif kind == "attnB":
                need |= {"wo_%d" % l}
        wspecs = [s for s in c.weight_specs() if s[0] in need]
        kinds = set(k for k, _ in steps)
    else:
        wspecs = [s for s in c.weight_specs() if int(s[0].split("_")[1]) < L]
        kinds = set()
    w_in, w_full = {}, {}
    for name, rows, cols in wspecs:
        if ag_weights:
            w_in[name] = nc.dram_tensor(name, [rows // NCORES, cols], F32, kind="ExternalInput").ap()
            w_full[name] = nc.dram_tensor(name + "_f", [rows, cols], F32).ap()
        else:
            w_full[name] = nc.dram_tensor(name, [rows, cols], F32, kind="ExternalInput").ap()
    W2 = c.GW - 127
    g2_d = nc.dram_tensor("gvec2", [H, 128, W2], BF16).ap()
    def xt(name, shape, dt, ext_in=False, ext_out=False):
        if ext_in:
            return nc.dram_tensor(name, shape, dt, kind="ExternalInput").ap()
        if ext_out:
            return nc.dram_tensor(name, shape, dt, kind="ExternalOutput").ap()
        return nc.dram_tensor(name, shape, dt).ap()
    halo_src = xt("halo_src", [D, HALO], F32)
    halo_dst = xt("halo_dst", [c.NB * D, HALO], F32, ext_in=(unf and "conv" in kinds))
    kx_src = xt("kx_src", [H * 128, T], BF16, ext_out=(unf and "attnA" in kinds))
    vx_src = xt("vx_src", [H * 128, T], BF16, ext_out=(unf and "attnA" in kinds))
    km_src = xt("km_src", [128, H * c.NBC], F32, ext_out=(unf and "attnA" in kinds))
    kx_dst = xt("kx_dst", [c.NB * H * 128, T], BF16, ext_in=(unf and "attnB" in kinds))
    vx_dst = xt("vx_dst", [c.NB * H * 128, T], BF16, ext_in=(unf and "attnB" in kinds))
    km_dst = xt("km_dst", [c.NB * 128, H * c.NBC], F32, ext_in=(unf and "attnB" in kinds))
    qt_out = xt("qt_out", [128, H * T], BF16, ext_out=True) if (unf and "attnA" in kinds) else None
    qt_in = xt("qt_in", [128, H * T], BF16, ext_in=True) if (unf and "attnB" in kinds) else None

    def sb(name, shape, dt):
        return es.enter_context(nc.sbuf_tensor(name, shape, dt))
    XT = sb("XT", [128, DC, T], F32)
    HT = sb("HT", [128, DC, T], BF16)
    UW = HALO + TN
    BIGN = max(FCH * T, 3 * DC * TN + 2 * UW, H * T + 8 * T)
    BIG = sb("BIG", [128, BIGN], BF16)
    WMAX = max(FCH, DC) * 128
    NWB = 6
    SCRN = max(NWB * WMAX, 3 * H * c.NBLK + 64 + 4 * c.S, 3 * c.GW)
    SCR = sb("SCR", [128, SCRN], BF16)
    PV = sb("PV", [128, c.NP], F32)
    CV = sb("CV", [128, c.NCV], F32)
    NTF, NTB = 4, 4
    TF = sb("TF", [128, NTF, TN], F32)
    TB = sb("TB", [128, NTB, TN], BF16)
    ONESD = sb("ONESD", [128, 128], F32)
    ONESH = sb("ONESH", [128, 128], F32)
    ONESB = sb("ONESB", [128, 128], BF16)
    IDB = sb("IDB", [128, 128], BF16)
    ESB = sb("ESB", [c.NBLK, c.NBLK * 128], BF16)
    EPST = sb("EPST", [128, 1], F32)
    HHT = sb("HHT", [128, DC, HALO], BF16)
    XH = sb("XH", [128, DC, HALO], F32)
    XHR = HT[:].rearrange("p c t -> p (c t)")[:, 0:2 * c.NB * DC * HALO].bitcast(F32).rearrange("p (r k) -> p r k", r=c.NB)
    SMALL = sb("SMALL", [128, 128], F32)
    S.fence_scratch = SMALL[:, 127:128]
    ALLHT = [("HT", cc, th) for cc in range(DC) for th in range(NTH)]
    BIG_TAGS = ("HID", "VV", "YY", "UE", "QT", "MT", "KNS", "VS", "BT")
    SCR_TAGS = ("W", "OHS", "GS", "KMA", "KMB", "KTA", "VTA")
    for t_ in BIG_TAGS:
        S.arena_tags[t_] = "BIG"
    for t_ in SCR_TAGS:
        S.arena_tags[t_] = "SCR"
    PS = [es.enter_context(nc.psum_tensor("ps%d" % i, [128, 512], F32)) for i in range(8)]

    tf_i = [0]
    tb_i = [0]

    def tf():
        i = tf_i[0] % NTF
        tf_i[0] += 1
        return ("TF", i), TF[:, i, :]

    def tb():
        i = tb_i[0] % NTB
        tb_i[0] += 1
        return ("TB", i), TB[:, i, :]

    def pvc(name, col, n=1):
        o = c.pv[name] + col
        return PV[:, o:o + n]

    S.emit("sp", lambda e: e.dma_start(out=PV[:], in_=pvec_d), writes=["PV"], dma="ld_pv")
    S.emit("sp", lambda e: e.dma_start(out=CV[:], in_=cvec_d), writes=["CV"], dma="ld_cv")
    S.emit("pool", lambda e: e.dma_start(out=IDB[:], in_=ident_d), writes=["IDB"], dma="ld_id")
    S.emit("pool", lambda e: e.dma_start(out=ESB[:], in_=esel_d), writes=["ESB"], dma="ld_es")
    ALLXT = [("XT", cc, th) for cc in range(DC) for th in range(NTH)]
    for cc in range(DC):
        S.emit("sp", lambda e, cc=cc: e.dma_start(out=XT[:, cc, :], in_=xT_d[cc * 128:(cc + 1) * 128, :]),
               writes=ALLXT, dma="ld_x")
    S.emit("dve", lambda e: e.memset(ONESD[:], 1.0 / D), writes=["ONES"])
    S.emit("dve", lambda e: e.memset(ONESH[:], 1.0 / 128), writes=["ONES"])
    S.emit("dve", lambda e: e.memset(ONESB[:], 1.0), writes=["ONES"])
    S.emit("dve", lambda e: e.memset(EPST[:], EPS), writes=["EPST"])

    wsrcs = {}
    if ag_weights:
        for name, rows, cols in wspecs:
            src = nc.dram_tensor(name + "_s", [rows // NCORES, cols], F32).ap()
            wsrcs[name] = src
            S.emit("pool", lambda e, src=src, name=name: e.dma_start(out=src, in_=w_in[name]),
                   writes=["wsrc_all"], dma="wcp")
        for name, rows, cols in wspecs:
            S.emit("pool", lambda e, name=name: e.collective_compute(
                "AllGather", ALU.bypass, replica_groups=groups_all, ins=[wsrcs[name]], outs=[w_full[name]]),
                reads=["wsrc_all"], writes=[("wfull", name)], dma="wag_" + name, inc=1)

    wslot = [0]

    def proj(wname, row0, OC, KC, segs, evac, ps_banks):
        wd = w_full[wname]
        nb = len(ps_banks)
        bi = 0
        for oc in range(OC):
            slot = wslot[0] % NWB
            wslot[0] += 1
            wt = SCR[:, slot * WMAX: slot * WMAX + KC * 128]
            S.emit("pool", lambda e, wt=wt, oc=oc: e.dma_start(
                out=wt, in_=wd[row0 + oc * 128: row0 + (oc + 1) * 128, 0:KC * 128]),
                reads=[("wfull", wname)], writes=[("W", slot)], dma="w%d" % slot)
            outs = []
            for (n, fn) in segs:
                b = ps_banks[bi % nb]
                bi += 1
                outs.append((("PS", b), PS[b][:, 0:n]))
            for kc in range(KC):
                for si, (n, fn) in enumerate(segs):
                    res, ap = fn(kc)
                    S.emit("pe", lambda e, o=outs[si][1], wt=wt, kc=kc, ap=ap, KC=KC: e.matmul(
                        o, lhsT=wt[:, kc * 128:(kc + 1) * 128], rhs=ap, start=(kc == 0), stop=(kc == KC - 1)),
                        reads=[("W", slot), res], writes=[outs[si][0]])
            evac(oc, outs)

    def rmsnorm(segs, gname, ps_banks, ones=None, nch=None):
        ones = ONESD if ones is None else ones
        nch = DC if nch is None else nch
        for si, (n, xin, hout) in enumerate(segs):
            b = ps_banks[si % len(ps_banks)]
            pst = PS[b][:, 0:n]
            for cc in range(nch):
                xr, xa = xin(cc)
                tr, ta = tf()
                S.emit("act", lambda e, ta=ta, xa=xa, n=n: e.activation(out=ta[:, 0:n], in_=xa, func=AF.Square),
                       reads=[xr], writes=[tr])
                S.emit("pe", lambda e, pst=pst, ta=ta, n=n, cc=cc: e.matmul(
                    pst, lhsT=ones[:], rhs=ta[:, 0:n], start=(cc == 0), stop=(cc == nch - 1)),
                    reads=[tr, "ONES"], writes=[("PS", b)])
            sr, sa = tf()
            S.emit("act", lambda e, sa=sa, pst=pst, n=n: e.activation(out=sa[:, 0:n], in_=pst, func=AF.Sqrt, bias=EPST[:]),
                   reads=[("PS", b), "EPST"], writes=[sr])
            rr, ra = tf()
            S.emit("dve", lambda e, ra=ra, sa=sa, n=n: e.reciprocal(out=ra[:, 0:n], in_=sa[:, 0:n]),
                   reads=[sr], writes=[rr])
            for cc in range(nch):
                xr, xa = xin(cc)
                hr, ha = hout(cc)
                S.emit("dve", lambda e, ha=ha, xa=xa, ra=ra, cc=cc, n=n: e.scalar_tensor_tensor(
                    out=ha, in0=xa, scalar=pvc(gname, cc), in1=ra[:, 0:n], op0=ALU.mult, op1=ALU.mult),
                    reads=[xr, rr, "PV"], writes=[hr])

    def xseg(th):
        return (TN,
                lambda cc, th=th: (("XT", cc, th), XT[:, cc, th * TN:(th + 1) * TN]),
                lambda cc, th=th: (("HT", cc, th), HT[:, cc, th * TN:(th + 1) * TN]))

    def hseg(th):
        return (TN, lambda kc, th=th: (("HT", kc, th), HT[:, kc, th * TN:(th + 1) * TN]))

    def ffn(tag, l):
        S.fence("BIG")
        _dbg("load")
        rmsnorm([xseg(th) for th in range(NTH)], "n%s_%d" % (tag, l), [0, 1])
        _dbg("norm")
        HID = BIG[:, 0:FCH * T]
        for hf in range(2):
            def evac_gu(oc, outs, hf=hf):
                if oc % 2 == 0:
                    evac_gu.gate = outs
                    return
                fcl = oc // 2
                for th in range(NTH):
                    (gr, ga), (ur, ua) = evac_gu.gate[th], outs[th]
                    tr, ta = tf()
                    S.emit("act", lambda e, ta=ta, ga=ga: e.activation(out=ta, in_=ga, func=AF.Silu),
                           reads=[gr], writes=[tr])
                    o = HID[:, fcl * T + th * TN: fcl * T + (th + 1) * TN]
                    S.emit("dve", lambda e, o=o, ta=ta, ua=ua: e.tensor_tensor(out=o, in0=ta, in1=ua, op=ALU.mult),
                           reads=[tr, ur], writes=[("HID", fcl, th)])
            proj("wgu%s_%d" % (tag, l), hf * FCH * 256, 2 * FCH, DC, [hseg(th) for th in range(NTH)], evac_gu,
                 list(range(8)))

            def evac_d(oc, outs):
                for th in range(NTH):
                    pr, pa = outs[th]
                    xa = XT[:, oc, th * TN:(th + 1) * TN]
                    S.emit("dve", lambda e, xa=xa, pa=pa: e.scalar_tensor_tensor(
                        out=xa, in0=pa, scalar=0.5, in1=xa, op0=ALU.mult, op1=ALU.add),
                        reads=[pr, ("XT", oc, th)], writes=[("XT", oc, th)])
            segs = [(TN, lambda kc, th=th: (("HID", kc, th), HID[:, kc * T + th * TN: kc * T + (th + 1) * TN]))
                    for th in range(NTH)]
            _dbg("gu%d" % hf)
            proj("wd%s_%d" % (tag, l), hf * D, DC, FCH, segs, evac_d, list(range(8)))
            _dbg("d%d" % hf)

    def conv_mixer(l):
        S.fence("BIG")
        for cc in range(DC if not unf else 0):
            S.emit("sp", lambda e, cc=cc: e.dma_start(out=halo_src[cc * 128:(cc + 1) * 128, :], in_=XT[:, cc, T - HALO:T]),
                   reads=[("XT", cc, NTH - 1)], writes=["halo_src"], dma="st_halo")
        if not unf:
            S.emit("pool", lambda e: e.collective_compute("AllGather", ALU.bypass, replica_groups=groups_b,
                                                          ins=[halo_src], outs=[halo_dst]),
                   reads=["halo_src"], writes=["halo_dst"], dma="ag_halo_%d" % l, inc=1)
        for r in range(c.NB):
            for cc in range(DC):
                S.emit("sp", lambda e, r=r, cc=cc: e.dma_start(
                    out=XHR[:, r, cc * HALO:(cc + 1) * HALO], in_=halo_dst[r * D + cc * 128: r * D + (cc + 1) * 128, :]),
                    reads=["halo_dst"], writes=["XHR"] + ALLHT, dma="ld_halo")
        XHf = XH[:].rearrange("p c h -> p (c h)")
        S.emit("dve", lambda e: e.tensor_scalar(out=XHf, in0=XHR[:, 0, :], scalar1=CV[:, c.cv_prevsel:c.cv_prevsel + 1],
                                                 scalar2=None, op0=ALU.mult),
               reads=["XHR", "CV"] + ALLHT, writes=["XH"])
        for r in range(1, c.NB):
            S.emit("dve", lambda e, r=r: e.scalar_tensor_tensor(
                out=XHf, in0=XHR[:, r, :], scalar=CV[:, c.cv_prevsel + r:c.cv_prevsel + r + 1], in1=XHf,
                op0=ALU.mult, op1=ALU.add), reads=["XHR", "CV", "XH"] + ALLHT, writes=["XH"])
        _dbg("halo")
        halo_seg = (HALO, lambda cc: ("XH", XH[:, cc, :]), lambda cc: ("HHT", HHT[:, cc, :]))
        import os
        v_ = os.environ.get("MKV", "0")
        if v_ == "0":
            rmsnorm([xseg(th) for th in range(NTH)] + [halo_seg], "nm_%d" % l, [0, 1, 2])
        elif v_ == "1":
            rmsnorm([xseg(th) for th in range(NTH)] + [halo_seg], "nm_%d" % l, [0, 1])
        elif v_ == "2":
            rmsnorm([xseg(th) for th in range(NTH)], "nm_%d" % l, [0, 1])
        elif v_ == "3":
            rmsnorm([halo_seg], "nm_%d" % l, [0, 1])
        elif v_ == "4":
            rmsnorm([halo_seg] + [xseg(th) for th in range(NTH)], "nm_%d" % l, [0, 1, 2])
        _dbg("cnorm")
        VV = BIG[:, 0:2 * DC * TN].bitcast(F32)
        YY = BIG[:, 2 * DC * TN: 3 * DC * TN]
        UE = BIG[:, 3 * DC * TN: 3 * DC * TN + 2 * UW].bitcast(F32)
        for th in range(NTH):
            def evac_pw1(oc, outs, th=th):
                if oc % 2 == 0:
                    evac_pw1.a = outs
                    return
                cc = oc // 2
                ub = 0
                ue = UE[:, ub * UW:(ub + 1) * UW]
                ures = ("UE", ub)
                for si, (lo, n) in enumerate([(0, HALO), (HALO, TN)]):
                    (ar, aa), (gr, ga) = evac_pw1.a[si], outs[si]
                    tr, ta = tf()
                    S.emit("act", lambda e, ta=ta, ga=ga, n=n, cc=cc: e.activation(
                        out=ta[:, 0:n], in_=ga, func=AF.Sigmoid, bias=pvc("pw1b_%d" % l, 2 * cc + 1)),
                        reads=[gr, "PV"], writes=[tr])
                    S.emit("dve", lambda e, ue=ue, lo=lo, n=n, aa=aa, ta=ta, cc=cc: e.scalar_tensor_tensor(
                        out=ue[:, lo:lo + n], in0=aa, scalar=pvc("pw1b_%d" % l, 2 * cc), in1=ta[:, 0:n],
                        op0=ALU.add, op1=ALU.mult), reads=[ar, tr, "PV"], writes=[ures])
                if th == 0:
                    S.emit("dve", lambda e, ue=ue: e.tensor_scalar(
                        out=ue[:, 0:HALO], in0=ue[:, 0:HALO], scalar1=CV[:, c.cv_hasprev:c.cv_hasprev + 1],
                        scalar2=None, op0=ALU.mult), reads=[ures, "CV"], writes=[ures])
                vo = VV[:, cc * TN:(cc + 1) * TN]
                vres = ("VV", cc)
                wcol = lambda k, cc=cc: pvc("dww_%d" % l, k * DC + cc)
                HN = TN // 2
                for hh_ in range(2):
                    S.emit("dve", lambda e, vo=vo, ue=ue, cc=cc, hh_=hh_: e.tensor_scalar(
                        out=vo[:, hh_ * HN:(hh_ + 1) * HN], in0=ue[:, 2 + hh_ * HN:2 + (hh_ + 1) * HN], scalar1=wcol(0),
                        scalar2=pvc("dwb_%d" % l, cc), op0=ALU.mult, op1=ALU.add), reads=[ures, "PV"], writes=[(vres, hh_)])
                for k in range(1, CONV_W):
                    for hh_ in range(2):
                        S.emit("dve", lambda e, vo=vo, ue=ue, k=k, hh_=hh_: e.scalar_tensor_tensor(
                            out=vo[:, hh_ * HN:(hh_ + 1) * HN], in0=ue[:, 2 + k + hh_ * HN:2 + k + (hh_ + 1) * HN], scalar=wcol(k),
                            in1=vo[:, hh_ * HN:(hh_ + 1) * HN], op0=ALU.mult, op1=ALU.add),
                            reads=[ures, (vres, hh_), "PV"], writes=[(vres, hh_)])
                S.emit("dve", lambda e: e.memset(S.fence_scratch, 0.0), reads=[(vres, 0), (vres, 1)], writes=[vres, "FSCR"])
            if th == 0:
                hs = (HALO, lambda kc: ("HHT", HHT[:, kc, :]))
            else:
                hs = (HALO, lambda kc, th=th: (("HT", kc, th - 1), HT[:, kc, th * TN - HALO: th * TN]))
            proj("wpw1_%d" % l, 0, 2 * DC, DC, [hs, hseg(th)], evac_pw1, [0, 1, 2, 3])
            _dbg("pw1_%d" % th)
            pm = PS[4][:, 0:TN]
            for cc in range(DC):
                S.emit("pe", lambda e, cc=cc: e.matmul(pm, lhsT=ONESD[:], rhs=VV[:, cc * TN:(cc + 1) * TN],
                                                        start=(cc == 0), stop=(cc == DC - 1)),
                       reads=[("VV", cc), "ONES"], writes=[("PS", 4)])
            mr, ma = tf()
            S.emit("act", lambda e, ma=ma: e.activation(out=ma, in_=pm, func=AF.Copy), reads=[("PS", 4)], writes=[mr])
            for cc in range(DC):
                vo = VV[:, cc * TN:(cc + 1) * TN]
                S.emit("dve", lambda e, vo=vo, ma=ma: e.tensor_tensor(out=vo, in0=vo, in1=ma, op=ALU.subtract),
                       reads=[("VV", cc), mr], writes=[("VV", cc)])
            pv_ = PS[5][:, 0:TN]
            for cc in range(DC):
                tr, ta = tf()
                S.emit("act", lambda e, ta=ta, cc=cc: e.activation(out=ta, in_=VV[:, cc * TN:(cc + 1) * TN], func=AF.Square),
                       reads=[("VV", cc)], writes=[tr])
                S.emit("pe", lambda e, ta=ta, cc=cc: e.matmul(pv_, lhsT=ONESD[:], rhs=ta, start=(cc == 0), stop=(cc == DC - 1)),
                       reads=[tr, "ONES"], writes=[("PS", 5)])
            sr, sa = tf()
            S.emit("act", lambda e, sa=sa: e.activation(out=sa, in_=pv_, func=AF.Sqrt, bias=EPST[:]),
                   reads=[("PS", 5), "EPST"], writes=[sr])
            rr, ra = tf()
            S.emit("dve", lambda e, ra=ra, sa=sa: e.reciprocal(out=ra, in_=sa), reads=[sr], writes=[rr])
            for cc in range(DC):
                vo = VV[:, cc * TN:(cc + 1) * TN]
                S.emit("dve", lambda e, vo=vo, ra=ra, cc=cc: e.scalar_tensor_tensor(
                    out=vo, in0=vo, scalar=pvc("lng_%d" % l, cc), in1=ra, op0=ALU.mult, op1=ALU.mult),
                    reads=[("VV", cc), rr, "PV"], writes=[("VV", cc)])
                S.emit("act", lambda e, vo=vo, cc=cc: e.activation(
                    out=YY[:, cc * TN:(cc + 1) * TN], in_=vo, func=AF.Silu, bias=pvc("lnb_%d" % l, cc)),
                    reads=[("VV", cc), "PV"], writes=[("YY", cc)])

            _dbg("ln_%d" % th)

            def evac_pw2(oc, outs, th=th):
                pr, pa = outs[0]
                xa = XT[:, oc, th * TN:(th + 1) * TN]
                S.emit("dve", lambda e, xa=xa, pa=pa, oc=oc: e.scalar_tensor_tensor(
                    out=xa, in0=pa, scalar=pvc("pw2b_%d" % l, oc), in1=xa, op0=ALU.add, op1=ALU.add),
                    reads=[pr, ("XT", oc, th), "PV"], writes=[("XT", oc, th)])
            proj("wpw2_%d" % l, 0, DC, DC, [(TN, lambda kc: (("YY", kc), YY[:, kc * TN:(kc + 1) * TN]))],
                 evac_pw2, [6, 7])
            _dbg("pw2_%d" % th)

    gvec_ready = [False]

    def build_gvec():
        S.fence("SCR")
        TABF = SMALL[0:33, 0:H]
        S.emit("dve", lambda e: e.memset(SMALL[0:64, 0:H], NEG), writes=["TAB"])
        S.emit("sp", lambda e: e.dma_start(out=SMALL[0:NBUCKET, 0:H], in_=relb_d), writes=["TAB"], dma="ld_tab")
        OHS = SCR[0:33, 0:2 * c.GW].bitcast(F32)
        S.emit("sp", lambda e: e.dma_start(out=OHS, in_=oh_d), writes=["OHS"], dma="ld_oh")
        GS = SCR[0:H, 2 * c.GW:3 * c.GW]
        for j0 in range(0, c.GW, 512):
            n = min(512, c.GW - j0)
            S.emit("pe", lambda e, j0=j0, n=n: e.matmul(PS[0][0:H, 0:n], lhsT=TABF, rhs=OHS[:, j0:j0 + n], start=True, stop=True),
                   reads=["TAB", "OHS"], writes=[("PS", 0)])
            S.emit("act", lambda e, j0=j0, n=n: e.activation(out=GS[:, j0:j0 + n], in_=PS[0][0:H, 0:n], func=AF.Copy),
                   reads=[("PS", 0)], writes=["GS"])
        for i in range(128):
            S.emit("sp", lambda e, i=i: e.dma_start(out=g2_d[:, i, :], in_=GS[:, 127 - i:127 - i + W2]),
                   reads=["GS"], writes=["gvec"], dma="st_gvec")
        S.fence("SCR")

    def attn_mixer(l, part="both"):
        NBLK, NBC, NQT, NKT = c.NBLK, c.NBC, c.NQT, c.NKT
        if part != "B":
            rmsnorm([xseg(th) for th in range(NTH)], "nm_%d" % l, [0, 1])
        QT = BIG[:, 0:H * T]
        MT = BIG[0:NBLK, H * T: H * T + 2 * T]
        KNS = BIG[:, H * T + 2 * T: H * T + 4 * T]
        VS = BIG[:, H * T + 4 * T: H * T + 6 * T]
        KM = SMALL[:, 32:32 + H * NBC]
        GQ = SMALL[:, 16:17]
        BTb = BIG[:, H * T + 6 * T: H * T + 8 * T]
        S.fence("BIG")
        if part == "B":
            S.emit("sp", lambda e: e.dma_start(out=QT, in_=qt_in),
                   writes=[("QT", h_, th_) for h_ in range(H) for th_ in range(NTH)], dma="ld_qt")
        S.emit("dve", lambda e: e.tensor_scalar(out=GQ, in0=pvc("qn_%d" % l, 0), scalar1=float(128 ** -0.5), scalar2=None,
                                                 op0=ALU.mult), reads=["PV"], writes=["GQ"])

        def evac_qkv(oc, outs):
            h, kind = oc // 3, oc % 3
            for th in range(NTH):
                pr, pa = outs[th]
                if kind == 2:
                    vr, va = tb()
                    S.emit("act", lambda e, va=va, pa=pa: e.activation(out=va, in_=pa, func=AF.Copy), reads=[pr], writes=[vr])
                    for tt in range(TN // 128):
                        qt = th * (TN // 128) + tt
                        b = 4 + (qt % 2)
                        ptb = PS[b][:].bitcast(BF16)[:, 0:128]
                        S.emit("pe", lambda e, ptb=ptb, va=va, tt=tt: e.transpose(ptb, va[:, tt * 128:(tt + 1) * 128], IDB[:]),
                               reads=[vr, "IDB"], writes=[("PS", b)])
                        vs = VS[:, (h % 2) * T + qt * 128:(h % 2) * T + (qt + 1) * 128]
                        S.emit("dve", lambda e, vs=vs, ptb=ptb: e.tensor_copy(out=vs, in_=ptb),
                               reads=[("PS", b)], writes=[("VS", h % 2)])
                    continue
                rr_, raw = tf()
                S.emit("act", lambda e, raw=raw, pa=pa: e.activation(out=raw, in_=pa, func=AF.Copy), reads=[pr], writes=[rr_])
                sr, sq = tf()
                S.emit("act", lambda e, sq=sq, pa=pa: e.activation(out=sq, in_=pa, func=AF.Square), reads=[pr], writes=[sr])
                b = 6 + (th % 2)
                S.emit("pe", lambda e, sq=sq, b=b: e.matmul(PS[b][:, 0:TN], lhsT=ONESH[:], rhs=sq, start=True, stop=True),
                       reads=[sr, "ONES"], writes=[("PS", b)])
                dr, sd = tf()
                S.emit("act", lambda e, sd=sd, b=b: e.activation(out=sd, in_=PS[b][:, 0:TN], func=AF.Sqrt, bias=EPST[:]),
                       reads=[("PS", b), "EPST"], writes=[dr])
                S.emit("dve", lambda e, sd=sd: e.reciprocal(out=sd, in_=sd), reads=[dr], writes=[dr])
                if kind == 0:
                    o = QT[:, h * T + th * TN: h * T + (th + 1) * TN]
                    S.emit("dve", lambda e, o=o, raw=raw, sd=sd: e.scalar_tensor_tensor(
                        out=o, in0=raw, scalar=GQ, in1=sd, op0=ALU.mult, op1=ALU.mult),
                        reads=[rr_, dr, "GQ"], writes=[("QT", h, th)])
                else:
                    o = KNS[:, (h % 2) * T + th * TN:(h % 2) * T + (th + 1) * TN]
                    S.emit("dve", lambda e, o=o, raw=raw, sd=sd: e.scalar_tensor_tensor(
                        out=o, in0=raw, scalar=pvc("kn_%d" % l, 0), in1=sd, op0=ALU.mult, op1=ALU.mult),
                        reads=[rr_, dr, "PV"], writes=[("KNS", h % 2)])
            if kind == 1:
                ks = KNS[:, (h % 2) * T:(h % 2 + 1) * T]
                S.emit("dve", lambda e, ks=ks, h=h: e.tensor_reduce(
                    out=KM[:, h * NBC:(h + 1) * NBC], in_=ks.rearrange("p (b k) -> p b k", k=BLK), axis=AX.X, op=ALU.add),
                    reads=[("KNS", h % 2)], writes=["KM"])
                S.emit("sp", lambda e, ks=ks, h=h: e.dma_start(out=kx_src[h * 128:(h + 1) * 128, :], in_=ks),
                       reads=[("KNS", h % 2)], writes=[("kx_src", h % 2)], dma="st_k%d" % (h % 2))
            if kind == 2:
                vs = VS[:, (h % 2) * T:(h % 2 + 1) * T]
                S.emit("sp", lambda e, vs=vs, h=h: e.dma_start(out=vx_src[h * 128:(h + 1) * 128, :], in_=vs),
                       reads=[("VS", h % 2)], writes=[("vx_src", h % 2)], dma="st_v%d" % (h % 2))
        _dbg("anorm")
        if part != "B":
            proj("wqkv_%d" % l, 0, 3 * H, DC, [hseg(th) for th in range(NTH)], evac_qkv, [0, 1, 2, 3])
            _dbg("qkv")
            S.emit("dve", lambda e: e.tensor_scalar(out=KM, in0=KM, scalar1=1.0 / BLK, scalar2=None, op0=ALU.mult),
                   reads=["KM"], writes=["KM"])
            S.emit("sp", lambda e: e.dma_start(out=km_src, in_=KM), reads=["KM"], writes=["km_src"], dma="st_km")
        if part == "A":
            S.emit("sp", lambda e: e.dma_start(out=qt_out, in_=QT),
                   reads=[("QT", h_, th_) for h_ in range(H) for th_ in range(NTH)], writes=["qt_out"], dma="st_qt")
            return
        if part == "both":
            for nm, src, dst in (("km", km_src, km_dst), ("kx", kx_src, kx_dst), ("vx", vx_src, vx_dst)):
                S.emit("pool", lambda e, src=src, dst=dst: e.collective_compute(
                    "AllGather", ALU.bypass, replica_groups=groups_b, ins=[src], outs=[dst]),
                    reads=[nm + "_src", (nm + "_src", 0), (nm + "_src", 1)], writes=[nm + "_dst"], dma="ag_%s_%d" % (nm, l), inc=1)
        S.fence("SCR")
        KMA = SCR[:, 0:2 * H * NBLK].bitcast(F32)
        KMB = SCR[:, 2 * H * NBLK: 3 * H * NBLK]
        for r in range(c.NB):
            dst = KMA.rearrange("p (h n) -> p h n", n=NBLK)[:, :, r * NBC:(r + 1) * NBC]
            S.emit("sp", lambda e, r=r, dst=dst: e.dma_start(
                out=dst, in_=km_dst[r * 128:(r + 1) * 128, :].rearrange("p (h n) -> p h n", n=NBC)),
                reads=["km_dst"], writes=["KMA"], dma="ld_kma")
        S.emit("dve", lambda e: e.tensor_copy(out=KMB, in_=KMA), reads=["KMA"], writes=["KMB"])
        _dbg("exch")
        o0 = 3 * H * NBLK
        o0 = (o0 + 63) // 64 * 64
        KTA = [SCR[:, o0 + i * c.S: o0 + (i + 1) * c.S] for i in range(2)]
        VTA = [SCR[:, o0 + (2 + i) * c.S: o0 + (3 + i) * c.S] for i in range(2)]
        BT = [BTb[:, i * T:(i + 1) * T] for i in range(2)]
        assert o0 + 4 * c.S <= SCRN, (o0 + 4 * c.S, SCRN)
        assert H * T + 8 * T <= BIGN
        bt_i = [0]
        for h in range(H):
            hb = h % 2
            for r in range(c.NB):
                S.emit("sp", lambda e, r=r, h=h, hb=hb: e.dma_start(
                    out=KTA[hb][:, r * T:(r + 1) * T], in_=kx_dst[(r * H + h) * 128:(r * H + h + 1) * 128, :]),
                    reads=["kx_dst"], writes=[("KTA", hb)], dma="ld_kta%d" % hb)
                S.emit("sp", lambda e, r=r, h=h, hb=hb: e.dma_start(
                    out=VTA[hb][:, r * T:(r + 1) * T], in_=vx_dst[(r * H + h) * 128:(r * H + h + 1) * 128, :]),
                    reads=["vx_dst"], writes=[("VTA", hb)], dma="ld_vta%d" % hb)
            mt = MT[:, hb * T:(hb + 1) * T]
            for qt in range(NQT):
                th, tq = qt // (TN // 128), qt % (TN // 128)
                qa = QT[:, h * T + qt * 128: h * T + (qt + 1) * 128]
                b = 6 + (qt % 2)
                S.emit("pe", lambda e, qa=qa, h=h, b=b: e.matmul(PS[b][:, 0:NBLK], lhsT=qa, rhs=KMB[:, h * NBLK:(h + 1) * NBLK],
                                                                 start=True, stop=True),
                       reads=[("QT", h, th), "KMB"], writes=[("PS", b)])
                gr, gt_ = tf()
                gm = gt_[:, 0:NBLK]
                m8 = gt_[:, 32:40]
                sel = gt_[:, 64:64 + NBLK]
                S.emit("dve", lambda e, gm=gm, b=b, qt=qt: e.tensor_tensor(
                    out=gm, in0=PS[b][:, 0:NBLK], in1=CV[:, c.cv_pastneg + qt * NBLK: c.cv_pastneg + (qt + 1) * NBLK], op=ALU.add),
                    reads=[("PS", b), "CV"], writes=[gr])
                S.emit("dve", lambda e, gm=gm, m8=m8: e.max(out=m8, in_=gm), reads=[gr], writes=[gr])
                S.emit("dve", lambda e, gm=gm, m8=m8, sel=sel: e.tensor_scalar(
                    out=sel, in0=gm, scalar1=m8[:, 2:3], scalar2=None, op0=ALU.is_ge), reads=[gr], writes=[gr])
                S.emit("dve", lambda e, sel=sel, qt=qt: e.tensor_tensor(
                    out=sel, in0=sel, in1=CV[:, c.cv_ownhot + qt * NBLK: c.cv_ownhot + (qt + 1) * NBLK], op=ALU.max),
                    reads=[gr, "CV"], writes=[gr])
                mr_, mb = tb()
                S.emit("dve", lambda e, sel=sel, mb=mb: e.tensor_scalar(
                    out=mb[:, 0:NBLK], in0=sel, scalar1=-1.0, scalar2=-NEG, op0=ALU.add, op1=ALU.mult),
                    reads=[gr], writes=[mr_])
                b2 = 4 + (qt % 2)
                ptb = PS[b2][:].bitcast(BF16)[0:NBLK, 0:128]
                S.emit("pe", lambda e, ptb=ptb, mb=mb: e.transpose(ptb, mb[:, 0:NBLK], IDB[:]),
                       reads=[mr_, "IDB"], writes=[("PS", b2)])
                S.emit("act", lambda e, ptb=ptb, mt=mt, qt=qt: e.activation(out=mt[:, qt * 128:(qt + 1) * 128], in_=ptb, func=AF.Copy),
                       reads=[("PS", b2)], writes=[("MT", hb)])
            _dbg("gate%d" % h)
            for kt in range(NKT):
                bi = bt_i[0] % 2
                bt_i[0] += 1
                m0 = c.S - 128 - kt * 128
                src = g2_d[h, :, m0:m0 + T]
                S.emit("sp", lambda e, bi=bi, src=src: e.dma_start(out=BT[bi], in_=src),
                       reads=["gvec"], writes=[("BT", bi)], dma="ld_bt%d" % bi)
                nblk = (kt * 128) // BLK
                for qc in range(NTH):
                    sb_ = (kt * NTH + qc) % 2
                    ps_s = PS[sb_][:, 0:TN]
                    qa = QT[:, h * T + qc * TN: h * T + (qc + 1) * TN]
                    S.emit("pe", lambda e, ps_s=ps_s, hb=hb, kt=kt, qa=qa: e.matmul(
                        ps_s, lhsT=KTA[hb][:, kt * 128:(kt + 1) * 128], rhs=qa, start=True, stop=False),
                        reads=[("KTA", hb), ("QT", h, qc)], writes=[("PS", sb_)])
                    S.emit("pe", lambda e, ps_s=ps_s, nblk=nblk, mt=mt, qc=qc: e.matmul(
                        ps_s, lhsT=ESB[:, nblk * 128:(nblk + 1) * 128], rhs=mt[:, qc * TN:(qc + 1) * TN], start=False, stop=True),
                        reads=[("MT", hb), "ESB"], writes=[("PS", sb_)])
                    tr, ta = tf()
                    S.emit("dve", lambda e, ta=ta, ps_s=ps_s, bi=bi, qc=qc: e.tensor_tensor(
                        out=ta, in0=ps_s, in1=BT[bi][:, qc * TN:(qc + 1) * TN], op=ALU.add),
                        reads=[("PS", sb_), ("BT", bi)], writes=[tr])
                    pr_, pt = tb()
                    S.emit("act", lambda e, pt=pt, ta=ta: e.activation(out=pt, in_=ta, func=AF.Exp), reads=[tr], writes=[pr_])
                    S.emit("pe", lambda e, pt=pt, hb=hb, kt=kt, qc=qc: e.matmul(
                        PS[2 + qc][:, 0:TN], lhsT=VTA[hb][:, kt * 128:(kt + 1) * 128], rhs=pt, start=(kt == 0), stop=(kt == NKT - 1)),
                        reads=[("VTA", hb), pr_], writes=[("PS", 2 + qc)])
                    S.emit("pe", lambda e, pt=pt, kt=kt, qc=qc: e.matmul(
                        PS[4 + qc][:, 0:TN], lhsT=ONESB[:], rhs=pt, start=(kt == 0), stop=(kt == NKT - 1)),
                        reads=[pr_, "ONES"], writes=[("PS", 4 + qc)])
            _dbg("core%d" % h)
            for qc in range(NTH):
                rr, ra = tf()
                S.emit("dve", lambda e, ra=ra, qc=qc: e.reciprocal(out=ra, in_=PS[4 + qc][:, 0:TN]),
                       reads=[("PS", 4 + qc)], writes=[rr])
                S.emit("dve", lambda e, ra=ra, qc=qc, h=h: e.tensor_tensor(
                    out=HT[:, h, qc * TN:(qc + 1) * TN], in0=PS[2 + qc][:, 0:TN], in1=ra, op=ALU.mult),
                    reads=[("PS", 2 + qc), rr], writes=[("HT", h, qc)])

        _dbg("heads")
        S.fence("SCR")

        def evac_o(oc, outs):
            for th in range(NTH):
                pr, pa = outs[th]
                xa = XT[:, oc, th * TN:(th + 1) * TN]
                S.emit("dve", lambda e, xa=xa, pa=pa: e.tensor_tensor(out=xa, in0=xa, in1=pa, op=ALU.add),
                       reads=[pr, ("XT", oc, th)], writes=[("XT", oc, th)])
        proj("wo_%d" % l, 0, DC, DC, [hseg(th) for th in range(NTH)], evac_o, [0, 1, 6, 7])

    phases = []
    if unf:
        phases = list(steps)
        if "attnB" in kinds:
            build_gvec()
    else:
        for l in range(L):
            phases.append(("ffn1", l))
            phases.append(("mix", l))
            phases.append(("ffn2", l))
        if L > 1:
            build_gvec()
    try:
      for kind, l in phases:
        if kind == "ffn1":
            ffn("1", l)
        elif kind == "ffn2":
            ffn("2", l)
        elif kind == "conv":
            conv_mixer(l)
        elif kind == "attnA":
            attn_mixer(l, "A")
        elif kind == "attnB":
            attn_mixer(l, "B")
        elif l % 2 == 0:
            conv_mixer(l)
        else:
            attn_mixer(l)
        if stop_after == (kind, l):
            break
    except _Stop:
        pass

    for cc in range(DC):
        S.emit("sp", lambda e, cc=cc: e.dma_start(out=yT_d[cc * 128:(cc + 1) * 128, :], in_=XT[:, cc, :]),
               reads=[("XT", cc, th) for th in range(NTH)], writes=["yT"], dma="st_y")
    with nc.Block() as block:
        S.finalize(block)
    es.close()
    return nc


def run(cfg, inputs, ag_weights=True, n_layers=None, stop_after=None, trace=False):
    shared, per_core = host_prepare(cfg, inputs)
    nc = build_program(cfg, ag_weights=ag_weights, n_layers=n_layers, stop_after=stop_after)
    L = cfg.L if n_layers is None else n_layers
    wnames = [s[0] for s in cfg.weight_specs() if int(s[0].split("_")[1]) < L]
    in_maps = []
    for core in range(NCORES):
        m = dict(per_core[core])
        for k in ("pvec", "relb", "ident", "esel"):
            m[k] = shared[k]
        for name in wnames:
            w = shared[name]
            if ag_weights:
                r = w.shape[0] // NCORES
                m[name] = w[core * r:(core + 1) * r]
            else:
                m[name] = w
        in_maps.append(m)
    res = run_bass_kernel_spmd(nc, in_maps, core_ids=list(range(NCORES)), trace=trace)
    B = NCORES // cfg.NB
    out = np.zeros((B, cfg.S, cfg.D), np.float32)
    for core in range(NCORES):
        b, cl = core // cfg.NB, core % cfg.NB
        out[b, cl * cfg.T:(cl + 1) * cfg.T, :] = res.results[core]["yT"].T
    return out, res


LAUNCHES = [
    [("ffn1", 0)],
    [("conv", 0), ("ffn2", 0), ("ffn1", 1), ("attnA", 1)],
    [("attnB", 1), ("ffn2", 1), ("ffn1", 2)],
    [("conv", 2), ("ffn2", 2), ("ffn1", 3), ("attnA", 3)],
    [("attnB", 3), ("ffn2", 3)],
]


def run_unfused(cfg, inputs, launches=None):
    c = cfg
    launches = LAUNCHES if launches is None else launches
    shared, per_core = host_prepare(cfg, inputs)
    xT = [per_core[i]["xT"] for i in range(NCORES)]
    carry = None
    for steps in launches:
        nc = build_program(cfg, steps=steps)
        kinds = set(k for k, _ in steps)
        need = set()
        for kind, l in steps:
            need |= {"ffn1": {"wgu1_%d" % l, "wd1_%d" % l}, "ffn2": {"wgu2_%d" % l, "wd2_%d" % l},
                     "conv": {"wpw1_%d" % l, "wpw2_%d" % l}, "attnA": {"wqkv_%d" % l}, "attnB": {"wo_%d" % l}}[kind]
        in_maps = []
        for core in range(NCORES):
            b = core // c.NB
            ranks = list(range(b * c.NB, (b + 1) * c.NB))
            m = {"xT": xT[core], "cvec": per_core[core]["cvec"], "oh": per_core[core]["oh"]}
            for k in ("pvec", "relb", "ident", "esel"):
                m[k] = shared[k]
            for name in need:
                m[name] = shared[name]
            if "conv" in kinds:
                m["halo_dst"] = np.concatenate([xT[r][:, c.T - HALO:] for r in ranks], axis=0)
            if "attnB" in kinds:
                m["qt_in"] = carry[core]["qt_out"]
                for nm in ("kx", "vx", "km"):
                    m[nm + "_dst"] = np.concatenate([carry[r][nm + "_src"] for r in ranks], axis=0)
            in_maps.append(m)
        res = run_bass_kernel_spmd(nc, in_maps, core_ids=list(range(NCORES)))
        xT = [res.results[i]["yT"] for i in range(NCORES)]
        carry = res.results
    B = NCORES // c.NB
    out = np.zeros((B, c.S, c.D), np.float32)
    for core in range(NCORES):
        b, cl = core // c.NB, core % c.NB
        out[b, cl * c.T:(cl + 1) * c.T, :] = xT[core].T
    return out


def kernel(**inputs):
    cfg = Cfg(**FULL_CFG)
    return run_unfused(cfg, inputs)
```

```python
import numpy as np
from contextlib import ExitStack
import concourse.bass as bass
import concourse.mybir as mybir
from concourse.bass_utils import run_bass_kernel_spmd

F32 = mybir.dt.float32
BF16 = mybir.dt.bfloat16
ALU = mybir.AluOpType
AF = mybir.ActivationFunctionType
AX = mybir.AxisListType

NCORES = 8
CONV_W = 31
HALO = 32
BLK = 256
NBUCKET = 32
EPS = 1e-6
NEG = -1.0e4

FULL_CFG = dict(D=2048, F=5632, T=1024, S=4096, L=4)
DEBUG_STOP = None
RELAX_SAME_ENGINE = True
import os
ALL_READERS = os.environ.get('MK_ALLR', '0') == '1'


class _Stop(Exception):
    pass


def _dbg(tag):
    if DEBUG_STOP == tag:
        raise _Stop()


class Sched:
    ENGS = ("pe", "act", "dve", "pool", "sp")

    def __init__(self, nc, es):
        self.nc = nc
        self.es = es
        self.insts = []
        self.last_w = {}
        self.readers = {}
        self.dma_sems = {}
        self.arena_tags = {}
        self.eng_sems = {e: es.enter_context(nc.semaphore("sem_" + e)) for e in self.ENGS}

    def _dma_sem(self, key):
        if key not in self.dma_sems:
            self.dma_sems[key] = [self.es.enter_context(self.nc.semaphore("dsem_%d" % len(self.dma_sems))), 0]
        return self.dma_sems[key]

    def emit(self, eng, fn, reads=(), writes=(), dma=None, inc=16):
        i = len(self.insts)
        deps = set()
        reads = list(reads)
        for k in list(reads) + list(writes):
            a = self.arena_tags.get(k[0] if isinstance(k, tuple) else k)
            if a is not None and ("EPOCH", a) not in reads and ("EPOCH", a) not in writes:
                reads.append(("EPOCH", a))
        for r in reads:
            if r in self.last_w:
                deps.add(self.last_w[r])
        for w in writes:
            if w in self.last_w:
                deps.add(self.last_w[w])
            for rd in self.readers.get(w, {}).values():
                deps.add(rd)
        deps.discard(i)
        rec = dict(eng=eng, fn=fn, deps=deps, dma=dma, inc=inc, token=None)
        if dma is not None:
            s = self._dma_sem(dma)
            s[1] += inc
            rec["token"] = (s[0], s[1])
        self.insts.append(rec)
        rkey = eng if dma is None else ("dma", dma)
        if ALL_READERS:
            rkey = i
        for r in reads:
            self.readers.setdefault(r, {})[rkey] = i
        for w in writes:
            self.last_w[w] = i
            self.readers[w] = {}
        return i

    def fence(self, arena):
        fs = self.fence_scratch
        self.emit("dve", lambda e: e.memset(fs, 0.0), writes=[("EPOCH", arena), "FSCR"])

    def finalize(self, block):
        insts = self.insts
        needed = set()
        pos = {}
        npos = {e: 0 for e in self.ENGS}
        for j, rec in enumerate(insts):
            pos[j] = npos[rec["eng"]]
            npos[rec["eng"]] += 1
        for j, rec in enumerate(insts):
            keep = set()
            for i in rec["deps"]:
                src = insts[i]
                if src["dma"] is None and src["eng"] == rec["eng"] and rec["eng"] == "pe" and rec["dma"] is None:
                    continue
                if RELAX_SAME_ENGINE and src["dma"] is None and rec["dma"] is None and src["eng"] == rec["eng"] and pos[j] - pos[i] >= 2:
                    continue
                keep.add(i)
                if src["dma"] is None:
                    needed.add(i)
            rec["deps"] = keep
        cnt = {e: 0 for e in self.ENGS}
        for i, rec in enumerate(insts):
            if rec["dma"] is None and i in needed:
                cnt[rec["eng"]] += 1
                rec["token"] = (self.eng_sems[rec["eng"]], cnt[rec["eng"]])
                rec["signal"] = True
            else:
                rec["signal"] = rec["dma"] is not None
        progs = {e: [] for e in self.ENGS}
        waited = {e: {} for e in self.ENGS}
        for rec in insts:
            e = rec["eng"]
            need = {}
            for i in rec["deps"]:
                sem, val = insts[i]["token"]
                k = id(sem)
                if waited[e].get(k, 0) >= val:
                    continue
                if k not in need or need[k][1] < val:
                    need[k] = (sem, val)
            for k, (sem, val) in need.items():
                waited[e][k] = val
            progs[e].append((list(need.values()), rec))

        def runner(items):
            def run(eng):
                for waits, rec in items:
                    for sem, val in waits:
                        eng.wait_ge(sem, val)
                    ins = rec["fn"](eng)
                    if rec["signal"]:
                        if rec["dma"] is not None:
                            if rec["inc"] == 1:
                                ins.then_inc(rec["token"][0])
                            else:
                                ins.then_inc(rec["token"][0], rec["inc"])
                        else:
                            ins.then_inc(rec["token"][0], 1)
            return run

        fin = []
        for key, (sem, val) in self.dma_sems.items():
            fin.append((sem, val))
        sp_items = progs["sp"]

        def sp_run(eng):
            runner(sp_items)(eng)
            for e2 in self.ENGS:
                if cnt[e2] > 0:
                    eng.wait_ge(self.eng_sems[e2], cnt[e2])
            for sem, val in fin:
                eng.wait_ge(sem, val)

        block.tensor(runner(progs["pe"]))
        block.scalar(runner(progs["act"]))
        block.vector(runner(progs["dve"]))
        block.gpsimd(runner(progs["pool"]))
        block.sync(sp_run)


def lay_w(w):
    K, N = w.shape
    kc, oc = K // 128, N // 128
    t = w.reshape(kc, 128, oc, 128)
    t = np.ascontiguousarray(t.transpose(2, 1, 0, 3))
    return t.reshape(oc * 128, kc * 128)


def interleave_cols(a, b):
    K, N = a.shape
    oc = N // 128
    t = np.stack([a.reshape(K, oc, 128), b.reshape(K, oc, 128)], axis=2)
    return t.reshape(K, 2 * N)


def fm(v):
    return np.ascontiguousarray(v.reshape(-1, 128).T)


def t5_bucket_np(d):
    max_exact = NBUCKET // 2
    n = np.maximum(d, 0)
    nf = np.maximum(n, 1).astype(np.float32)
    large = max_exact + (np.log(nf / np.float32(max_exact)) / np.float32(np.log(2048 / max_exact))
                         * np.float32(NBUCKET - max_exact)).astype(np.int32)
    large = np.minimum(large, NBUCKET - 1)
    return np.where(n < max_exact, n, large)


class Cfg:
    def __init__(self, D, F, T, S, L):
        self.D, self.F, self.T, self.S, self.L = D, F, T, S, L
        self.DC = D // 128
        self.FC = F // 128
        self.FCH = self.FC // 2
        self.H = D // 128
        self.NB = S // T
        self.NG = NCORES // self.NB
        self.TN = min(512, T)
        self.NTH = T // self.TN
        self.NBLK = S // BLK
        self.NBC = T // BLK
        self.NQT = T // 128
        self.NKT = S // 128
        self.GW = S + T
        off = 0
        self.pv = {}

        def add(name, n):
            nonlocal off
            self.pv[name] = off
            off += n
        for l in range(L):
            add("n1_%d" % l, self.DC)
            add("nm_%d" % l, self.DC)
            add("n2_%d" % l, self.DC)
            if l % 2 == 0:
                add("pw1b_%d" % l, 2 * self.DC)
                add("dww_%d" % l, CONV_W * self.DC)
                add("dwb_%d" % l, self.DC)
                add("lng_%d" % l, self.DC)
                add("lnb_%d" % l, self.DC)
                add("pw2b_%d" % l, self.DC)
            else:
                add("qn_%d" % l, 1)
                add("kn_%d" % l, 1)
        self.NP = off
        self.cv_hasprev = 0
        self.cv_prevsel = 1
        self.cv_pastneg = 1 + self.NB
        self.cv_ownhot = self.cv_pastneg + self.NQT * self.NBLK
        self.NCV = self.cv_ownhot + self.NQT * self.NBLK

    def weight_specs(self):
        sp = []
        for l in range(self.L):
            sp.append(("wgu1_%d" % l, 2 * self.F, self.D))
            sp.append(("wd1_%d" % l, 2 * self.D, self.F // 2))
            if l % 2 == 0:
                sp.append(("wpw1_%d" % l, 2 * self.D, self.D))
                sp.append(("wpw2_%d" % l, self.D, self.D))
            else:
                sp.append(("wqkv_%d" % l, 3 * self.D, self.D))
                sp.append(("wo_%d" % l, self.D, self.D))
            sp.append(("wgu2_%d" % l, 2 * self.F, self.D))
            sp.append(("wd2_%d" % l, 2 * self.D, self.F // 2))
        return sp


def host_prepare(cfg, inputs):
    c = cfg
    g = {k: np.asarray(v, dtype=np.float32) for k, v in inputs.items()}
    shared = {}
    for l in range(c.L):
        for tag in ("1", "2"):
            wg, wu, wd = g["ffn%s_w_gate" % tag][l], g["ffn%s_w_up" % tag][l], g["ffn%s_w_down" % tag][l]
            shared["wgu%s_%d" % (tag, l)] = lay_w(interleave_cols(wg, wu))
            hf = c.F // 2
            shared["wd%s_%d" % (tag, l)] = np.concatenate([lay_w(wd[:hf]), lay_w(wd[hf:])], axis=0)
        j = l // 2
        if l % 2 == 0:
            w1 = g["conv_pw1_w"][j]
            shared["wpw1_%d" % l] = lay_w(interleave_cols(w1[:, :c.D], w1[:, c.D:]))
            shared["wpw2_%d" % l] = lay_w(g["conv_pw2_w"][j])
        else:
            wq = g["attn_wqkv"][j]
            K = wq.shape[0]
            t = np.stack([wq[:, i * c.D:(i + 1) * c.D].reshape(K, c.H, 128) for i in range(3)], axis=2)
            shared["wqkv_%d" % l] = lay_w(t.reshape(K, 3 * c.D))
            shared["wo_%d" % l] = lay_w(g["attn_wo"][j])
    pvec = np.zeros((128, c.NP), np.float32)

    def put(name, arr):
        o = c.pv[name]
        pvec[:, o:o + arr.shape[1]] = arr
    for l in range(c.L):
        put("n1_%d" % l, fm(g["ffn1_norm"][l]))
        put("nm_%d" % l, fm(g["mix_norm"][l]))
        put("n2_%d" % l, fm(g["ffn2_norm"][l]))
        j = l // 2
        if l % 2 == 0:
            b1 = g["conv_pw1_b"][j]
            ba, bg = fm(b1[:c.D]), fm(b1[c.D:])
            put("pw1b_%d" % l, np.stack([ba, bg], axis=2).reshape(128, 2 * c.DC))
            dw = g["conv_dw_w"][j]
            put("dww_%d" % l, np.ascontiguousarray(dw.reshape(CONV_W, c.DC, 128).transpose(2, 0, 1)).reshape(128, CONV_W * c.DC))
            put("dwb_%d" % l, fm(g["conv_dw_b"][j]))
            put("lng_%d" % l, fm(g["conv_ln_g"][j]))
            put("lnb_%d" % l, fm(g["conv_ln_b"][j]))
            put("pw2b_%d" % l, fm(g["conv_pw2_b"][j]))
        else:
            put("qn_%d" % l, g["attn_q_norm"][j].reshape(128, 1))
            put("kn_%d" % l, g["attn_k_norm"][j].reshape(128, 1))
    shared["pvec"] = pvec
    shared["relb"] = g["rel_bias"]
    shared["ident"] = np.eye(128, dtype=np.float32)
    es = np.zeros((c.NBLK, c.NBLK, 128), np.float32)
    for n in range(c.NBLK):
        es[n, n, :] = 1.0
    shared["esel"] = es.reshape(c.NBLK, c.NBLK * 128)
    x = g["x"]
    per_core = []
    for core in range(NCORES):
        b, cl = core // c.NB, core % c.NB
        d = {}
        d["xT"] = np.ascontiguousarray(x[b, cl * c.T:(cl + 1) * c.T, :].T)
        cv = np.zeros((128, c.NCV), np.float32)
        cv[:, c.cv_hasprev] = 1.0 if cl > 0 else 0.0
        if cl > 0:
            cv[:, c.cv_prevsel + cl - 1] = 1.0
        for qt in range(c.NQT):
            own = cl * c.NBC + (qt * 128) // BLK
            for n in range(c.NBLK):
                cv[:, c.cv_pastneg + qt * c.NBLK + n] = 0.0 if n < own else -3.0e38
                cv[:, c.cv_ownhot + qt * c.NBLK + n] = 1.0 if n == own else 0.0
        d["cvec"] = cv
        jj = np.arange(c.GW)
        dist = jj - (c.S - 1) + cl * c.T
        oh = np.zeros((33, c.GW), np.float32)
        bk = t5_bucket_np(dist.astype(np.int64))
        oh[bk[dist >= 0], jj[dist >= 0]] = 1.0
        oh[32, jj[dist < 0]] = 1.0
        d["oh"] = oh
        per_core.append(d)
    return shared, per_core


def build_program(cfg, ag_weights=True, n_layers=None, stop_after=None, steps=None):
    c = cfg
    L = c.L if n_layers is None else n_layers
    D, F, T, DC, FC, FCH, H, TN, NTH = c.D, c.F, c.T, c.DC, c.FC, c.FCH, c.H, c.TN, c.NTH
    nc = bass.Bass("TRN2", target_bir_lowering=False)
    es = ExitStack()
    S = Sched(nc, es)
    groups_all = [list(range(NCORES))]
    groups_b = [list(range(b * c.NB, (b + 1) * c.NB)) for b in range(c.NG)]

    xT_d = nc.dram_tensor("xT", [D, T], F32, kind="ExternalInput").ap()
    yT_d = nc.dram_tensor("yT", [D, T], F32, kind="ExternalOutput").ap()
    pvec_d = nc.dram_tensor("pvec", [128, c.NP], F32, kind="ExternalInput").ap()
    cvec_d = nc.dram_tensor("cvec", [128, c.NCV], F32, kind="ExternalInput").ap()
    relb_d = nc.dram_tensor("relb", [NBUCKET, H], F32, kind="ExternalInput").ap()
    ident_d = nc.dram_tensor("ident", [128, 128], F32, kind="ExternalInput").ap()
    esel_d = nc.dram_tensor("esel", [c.NBLK, c.NBLK * 128], F32, kind="ExternalInput").ap()
    oh_d = nc.dram_tensor("oh", [33, c.GW], F32, kind="ExternalInput").ap()
    unf = steps is not None
    if unf:
        ag_weights = False
        need = set()
        for kind, l in steps:
            if kind == "ffn1":
                need |= {"wgu1_%d" % l, "wd1_%d" % l}
            elif kind == "ffn2":
                need |= {"wgu2_%d" % l, "wd2_%d" % l}
            elif kind == "conv":
                need |= {"wpw1_%d" % l, "wpw2_%d" % l}
            elif kind == "attnA":
                need |= {"wqkv_%d" % l}
            elif kind == "attnB":
                need |= {"wo_%d" % l}
        wspecs = [s for s in c.weight_specs() if s[0] in need]
        kinds = set(k for k, _ in steps)
    else:
        wspecs = [s for s in c.weight_specs() if int(s[0].split("_")[1]) < L]
        kinds = set()
    w_in, w_full = {}, {}
    for name, rows, cols in wspecs:
        if ag_weights:
            w_in[name] = nc.dram_tensor(name, [rows // NCORES, cols], F32, kind="ExternalInput").ap()
            w_full[name] = nc.dram_tensor(name + "_f", [rows, cols], F32).ap()
        else:
            w_full[name] = nc.dram_tensor(name, [rows, cols], F32, kind="ExternalInput").ap()
    W2 = c.GW - 127
    g2_d = nc.dram_tensor("gvec2", [H, 128, W2], BF16).ap()
    def xt(name, shape, dt, ext_in=False, ext_out=False):
        if ext_in:
            return nc.dram_tensor(name, shape, dt, kind="ExternalInput").ap()
        if ext_out:
            return nc.dram_tensor(name, shape, dt, kind="ExternalOutput").ap()
        return nc.dram_tensor(name, shape, dt).ap()
    halo_src = xt("halo_src", [D, HALO], F32)
    halo_dst = xt("halo_dst", [c.NB * D, HALO], F32, ext_in=(unf and "conv" in kinds))
    kx_src = xt("kx_src", [H * 128, T], BF16, ext_out=(unf and "attnA" in kinds))
    vx_src = xt("vx_src", [H * 128, T], BF16, ext_out=(unf and "attnA" in kinds))
    km_src = xt("km_src", [128, H * c.NBC], F32, ext_out=(unf and "attnA" in kinds))
    kx_dst = xt("kx_dst", [c.NB * H * 128, T], BF16, ext_in=(unf and "attnB" in kinds))
    vx_dst = xt("vx_dst", [c.NB * H * 128, T], BF16, ext_in=(unf and "attnB" in kinds))
    km_dst = xt("km_dst", [c.NB * 128, H * c.NBC], F32, ext_in=(unf and "attnB" in kinds))
    qt_out = xt("qt_out", [128, H * T], BF16, ext_out=True) if (unf and "attnA" in kinds) else None
    qt_in = xt("qt_in", [128, H * T], BF16, ext_in=True) if (unf and "attnB" in kinds) else None

    def sb(name, shape, dt):
        return es.enter_context(nc.sbuf_tensor(name, shape, dt))
    XT = sb("XT", [128, DC, T], F32)
    HT = sb("HT", [128, DC, T], BF16)
    UW = HALO + TN
    BIGN = max(FCH * T, 3 * DC * TN + 2 * UW, H * T + 8 * T)
    BIG = sb("BIG", [128, BIGN], BF16)
    WMAX = max(FCH, DC) * 128
    NWB = 6
    SCRN = max(NWB * WMAX, 3 * H * c.NBLK + 64 + 4 * c.S, 3 * c.GW)
    SCR = sb("SCR", [128, SCRN], BF16)
    PV = sb("PV", [128, c.NP], F32)
    CV = sb("CV", [128, c.NCV], F32)
    NTF, NTB = 4, 4
    TF = sb("TF", [128, NTF, TN], F32)
    TB = sb("TB", [128, NTB, TN], BF16)
    ONESD = sb("ONESD", [128, 128], F32)
    ONESH = sb("ONESH", [128, 128], F32)
    ONESB = sb("ONESB", [128, 128], BF16)
    IDB = sb("IDB", [128, 128], BF16)
    ESB = sb("ESB", [c.NBLK, c.NBLK * 128], BF16)
    EPST = sb("EPST", [128, 1], F32)
    HHT = sb("HHT", [128, DC, HALO], BF16)
    XH = sb("XH", [128, DC, HALO], F32)
    XHR = HT[:].rearrange("p c t -> p (c t)")[:, 0:2 * c.NB * DC * HALO].bitcast(F32).rearrange("p (r k) -> p r k", r=c.NB)
    SMALL = sb("SMALL", [128, 128], F32)
    S.fence_scratch = SMALL[:, 127:128]
    ALLHT = [("HT", cc, th) for cc in range(DC) for th in range(NTH)]
    BIG_TAGS = ("HID", "VV", "YY", "UE", "QT", "MT", "KNS", "VS", "BT")
    SCR_TAGS = ("W", "OHS", "GS", "KMA", "KMB", "KTA", "VTA")
    for t_ in BIG_TAGS:
        S.arena_tags[t_] = "BIG"
    for t_ in SCR_TAGS:
        S.arena_tags[t_] = "SCR"
    PS = [es.enter_context(nc.psum_tensor("ps%d" % i, [128, 512], F32)) for i in range(8)]

    tf_i = [0]
    tb_i = [0]

    def tf():
        i = tf_i[0] % NTF
        tf_i[0] += 1
        return ("TF", i), TF[:, i, :]

    def tb():
        i = tb_i[0] % NTB
        tb_i[0] += 1
        return ("TB", i), TB[:, i, :]

    def pvc(name, col, n=1):
        o = c.pv[name] + col
        return PV[:, o:o + n]

    S.emit("sp", lambda e: e.dma_start(out=PV[:], in_=pvec_d), writes=["PV"], dma="ld_pv")
    S.emit("sp", lambda e: e.dma_start(out=CV[:], in_=cvec_d), writes=["CV"], dma="ld_cv")
    S.emit("pool", lambda e: e.dma_start(out=IDB[:], in_=ident_d), writes=["IDB"], dma="ld_id")
    S.emit("pool", lambda e: e.dma_start(out=ESB[:], in_=esel_d), writes=["ESB"], dma="ld_es")
    ALLXT = [("XT", cc, th) for cc in range(DC) for th in range(NTH)]
    for cc in range(DC):
        S.emit("sp", lambda e, cc=cc: e.dma_start(out=XT[:, cc, :], in_=xT_d[cc * 128:(cc + 1) * 128, :]),
               writes=ALLXT, dma="ld_x")
    S.emit("dve", lambda e: e.memset(ONESD[:], 1.0 / D), writes=["ONES"])
    S.emit("dve", lambda e: e.memset(ONESH[:], 1.0 / 128), writes=["ONES"])
    S.emit("dve", lambda e: e.memset(ONESB[:], 1.0), writes=["ONES"])
    S.emit("dve", lambda e: e.memset(EPST[:], EPS), writes=["EPST"])

    wsrcs = {}
    if ag_weights:
        for name, rows, cols in wspecs:
            src = nc.dram_tensor(name + "_s", [rows // NCORES, cols], F32).ap()
            wsrcs[name] = src
            S.emit("pool", lambda e, src=src, name=name: e.dma_start(out=src, in_=w_in[name]),
                   writes=["wsrc_all"], dma="wcp")
        for name, rows, cols in wspecs:
            S.emit("pool", lambda e, name=name: e.collective_compute(
                "AllGather", ALU.bypass, replica_groups=groups_all, ins=[wsrcs[name]], outs=[w_full[name]]),
                reads=["wsrc_all"], writes=[("wfull", name)], dma="wag_" + name, inc=1)

    wslot = [0]

    def proj(wname, row0, OC, KC, segs, evac, ps_banks):
        wd = w_full[wname]
        nb = len(ps_banks)
        bi = 0
        for oc in range(OC):
            slot = wslot[0] % NWB
            wslot[0] += 1
            wt = SCR[:, slot * WMAX: slot * WMAX + KC * 128]
            S.emit("pool", lambda e, wt=wt, oc=oc: e.dma_start(
                out=wt, in_=wd[row0 + oc * 128: row0 + (oc + 1) * 128, 0:KC * 128]),
                reads=[("wfull", wname)], writes=[("W", slot)], dma="w%d" % slot)
            outs = []
            for (n, fn) in segs:
                b = ps_banks[bi % nb]
                bi += 1
                outs.append((("PS", b), PS[b][:, 0:n]))
            for kc in range(KC):
                for si, (n, fn) in enumerate(segs):
                    res, ap = fn(kc)
                    S.emit("pe", lambda e, o=outs[si][1], wt=wt, kc=kc, ap=ap, KC=KC: e.matmul(
                        o, lhsT=wt[:, kc * 128:(kc + 1) * 128], rhs=ap, start=(kc == 0), stop=(kc == KC - 1)),
                        reads=[("W", slot), res], writes=[outs[si][0]])
            evac(oc, outs)

    def rmsnorm(segs, gname, ps_banks, ones=None, nch=None):
        ones = ONESD if ones is None else ones
        nch = DC if nch is None else nch
        for si, (n, xin, hout) in enumerate(segs):
            b = ps_banks[si % len(ps_banks)]
            pst = PS[b][:, 0:n]
            for cc in range(nch):
                xr, xa = xin(cc)
                tr, ta = tf()
                S.emit("act", lambda e, ta=ta, xa=xa, n=n: e.activation(out=ta[:, 0:n], in_=xa, func=AF.Square),
                       reads=[xr], writes=[tr])
                S.emit("pe", lambda e, pst=pst, ta=ta, n=n, cc=cc: e.matmul(
                    pst, lhsT=ones[:], rhs=ta[:, 0:n], start=(cc == 0), stop=(cc == nch - 1)),
                    reads=[tr, "ONES"], writes=[("PS", b)])
            sr, sa = tf()
            S.emit("act", lambda e, sa=sa, pst=pst, n=n: e.activation(out=sa[:, 0:n], in_=pst, func=AF.Sqrt, bias=EPST[:]),
                   reads=[("PS", b), "EPST"], writes=[sr])
            rr, ra = tf()
            S.emit("dve", lambda e, ra=ra, sa=sa, n=n: e.reciprocal(out=ra[:, 0:n], in_=sa[:, 0:n]),
                   reads=[sr], writes=[rr])
            for cc in range(nch):
                xr, xa = xin(cc)
                hr, ha = hout(cc)
                S.emit("dve", lambda e, ha=ha, xa=xa, ra=ra, cc=cc, n=n: e.scalar_tensor_tensor(
                    out=ha, in0=xa, scalar=pvc(gname, cc), in1=ra[:, 0:n], op0=ALU.mult, op1=ALU.mult),
                    reads=[xr, rr, "PV"], writes=[hr])

    def xseg(th):
        return (TN,
                lambda cc, th=th: (("XT", cc, th), XT[:, cc, th * TN:(th + 1) * TN]),
                lambda cc, th=th: (("HT", cc, th), HT[:, cc, th * TN:(th + 1) * TN]))

    def hseg(th):
        return (TN, lambda kc, th=th: (("HT", kc, th), HT[:, kc, th * TN:(th + 1) * TN]))

    def ffn(tag, l):
        S.fence("BIG")
        _dbg("load")
        rmsnorm([xseg(th) for th in range(NTH)], "n%s_%d" % (tag, l), [0, 1])
        _dbg("norm")
        HID = BIG[:, 0:FCH * T]
        for hf in range(2):
            def evac_gu(oc, outs, hf=hf):
                if oc % 2 == 0:
                    evac_gu.gate = outs
                    return
                fcl = oc // 2
                for th in range(NTH):
                    (gr, ga), (ur, ua) = evac_gu.gate[th], outs[th]
                    tr, ta = tf()
                    S.emit("act", lambda e, ta=ta, ga=ga: e.activation(out=ta, in_=ga, func=AF.Silu),
                           reads=[gr], writes=[tr])
                    o = HID[:, fcl * T + th * TN: fcl * T + (th + 1) * TN]
                    S.emit("dve", lambda e, o=o, ta=ta, ua=ua: e.tensor_tensor(out=o, in0=ta, in1=ua, op=ALU.mult),
                           reads=[tr, ur], writes=[("HID", fcl, th)])
            proj("wgu%s_%d" % (tag, l), hf * FCH * 256, 2 * FCH, DC, [hseg(th) for th in range(NTH)], evac_gu,
                 list(range(8)))

            def evac_d(oc, outs):
                for th in range(NTH):
                    pr, pa = outs[th]
                    xa = XT[:, oc, th * TN:(th + 1) * TN]
                    S.emit("dve", lambda e, xa=xa, pa=pa: e.scalar_tensor_tensor(
                        out=xa, in0=pa, scalar=0.5, in1=xa, op0=ALU.mult, op1=ALU.add),
                        reads=[pr, ("XT", oc, th)], writes=[("XT", oc, th)])
            segs = [(TN, lambda kc, th=th: (("HID", kc, th), HID[:, kc * T + th * TN: kc * T + (th + 1) * TN]))
                    for th in range(NTH)]
            _dbg("gu%d" % hf)
            proj("wd%s_%d" % (tag, l), hf * D, DC, FCH, segs, evac_d, list(range(8)))
            _dbg("d%d" % hf)

    def conv_mixer(l):
        S.fence("BIG")
        for cc in range(DC if not unf else 0):
            S.emit("sp", lambda e, cc=cc: e.dma_start(out=halo_src[cc * 128:(cc + 1) * 128, :], in_=XT[:, cc, T - HALO:T]),
                   reads=[("XT", cc, NTH - 1)], writes=["halo_src"], dma="st_halo")
        if not unf:
            S.emit("pool", lambda e: e.collective_compute("AllGather", ALU.bypass, replica_groups=groups_b,
                                                          ins=[halo_src], outs=[halo_dst]),
                   reads=["halo_src"], writes=["halo_dst"], dma="ag_halo_%d" % l, inc=1)
        for r in range(c.NB):
            for cc in range(DC):
                S.emit("sp", lambda e, r=r, cc=cc: e.dma_start(
                    out=XHR[:, r, cc * HALO:(cc + 1) * HALO], in_=halo_dst[r * D + cc * 128: r * D + (cc + 1) * 128, :]),
                    reads=["halo_dst"], writes=["XHR"] + ALLHT, dma="ld_halo")
        XHf = XH[:].rearrange("p c h -> p (c h)")
        S.emit("dve", lambda e: e.tensor_scalar(out=XHf, in0=XHR[:, 0, :], scalar1=CV[:, c.cv_prevsel:c.cv_prevsel + 1],
                                                 scalar2=None, op0=ALU.mult),
               reads=["XHR", "CV"] + ALLHT, writes=["XH"])
        for r in range(1, c.NB):
            S.emit("dve", lambda e, r=r: e.scalar_tensor_tensor(
                out=XHf, in0=XHR[:, r, :], scalar=CV[:, c.cv_prevsel + r:c.cv_prevsel + r + 1], in1=XHf,
                op0=ALU.mult, op1=ALU.add), reads=["XHR", "CV", "XH"] + ALLHT, writes=["XH"])
        _dbg("halo")
        halo_seg = (HALO, lambda cc: ("XH", XH[:, cc, :]), lambda cc: ("HHT", HHT[:, cc, :]))
        import os
        v_ = os.environ.get("MKV", "0")
        if v_ == "0":
            rmsnorm([xseg(th) for th in range(NTH)] + [halo_seg], "nm_%d" % l, [0, 1, 2])
        elif v_ == "1":
            rmsnorm([xseg(th) for th in range(NTH)] + [halo_seg], "nm_%d" % l, [0, 1])
        elif v_ == "2":
            rmsnorm([xseg(th) for th in range(NTH)], "nm_%d" % l, [0, 1])
        elif v_ == "3":
            rmsnorm([halo_seg], "nm_%d" % l, [0, 1])
        elif v_ == "4":
            rmsnorm([halo_seg] + [xseg(th) for th in range(NTH)], "nm_%d" % l, [0, 1, 2])
        _dbg("cnorm")
        VV = BIG[:, 0:2 * DC * TN].bitcast(F32)
        YY = BIG[:, 2 * DC * TN: 3 * DC * TN]
        UE = BIG[:, 3 * DC * TN: 3 * DC * TN + 2 * UW].bitcast(F32)
        for th in range(NTH):
            def evac_pw1(oc, outs, th=th):
                if oc % 2 == 0:
                    evac_pw1.a = outs
                    return
                cc = oc // 2
                ub = 0
                ue = UE[:, ub * UW:(ub + 1) * UW]
                ures = ("UE", ub)
                for si, (lo, n) in enumerate([(0, HALO), (HALO, TN)]):
                    (ar, aa), (gr, ga) = evac_pw1.a[si], outs[si]
                    tr, ta = tf()
                    S.emit("act", lambda e, ta=ta, ga=ga, n=n, cc=cc: e.activation(
                        out=ta[:, 0:n], in_=ga, func=AF.Sigmoid, bias=pvc("pw1b_%d" % l, 2 * cc + 1)),
                        reads=[gr, "PV"], writes=[tr])
                    S.emit("dve", lambda e, ue=ue, lo=lo, n=n, aa=aa, ta=ta, cc=cc: e.scalar_tensor_tensor(
                        out=ue[:, lo:lo + n], in0=aa, scalar=pvc("pw1b_%d" % l, 2 * cc), in1=ta[:, 0:n],
                        op0=ALU.add, op1=ALU.mult), reads=[ar, tr, "PV"], writes=[ures])
                if th == 0:
                    S.emit("dve", lambda e, ue=ue: e.tensor_scalar(
                        out=ue[:, 0:HALO], in0=ue[:, 0:HALO], scalar1=CV[:, c.cv_hasprev:c.cv_hasprev + 1],
                        scalar2=None, op0=ALU.mult), reads=[ures, "CV"], writes=[ures])
                vo = VV[:, cc * TN:(cc + 1) * TN]
                vres = ("VV", cc)
                wcol = lambda k, cc=cc: pvc("dww_%d" % l, k * DC + cc)
                HN = TN // 2
                for hh_ in range(2):
                    S.emit("dve", lambda e, vo=vo, ue=ue, cc=cc, hh_=hh_: e.tensor_scalar(
                        out=vo[:, hh_ * HN:(hh_ + 1) * HN], in0=ue[:, 2 + hh_ * HN:2 + (hh_ + 1) * HN], scalar1=wcol(0),
                        scalar2=pvc("dwb_%d" % l, cc), op0=ALU.mult, op1=ALU.add), reads=[ures, "PV"], writes=[(vres, hh_)])
                for k in range(1, CONV_W):
                    for hh_ in range(2):
                        S.emit("dve", lambda e, vo=vo, ue=ue, k=k, hh_=hh_: e.scalar_tensor_tensor(
                            out=vo[:, hh_ * HN:(hh_ + 1) * HN], in0=ue[:, 2 + k + hh_ * HN:2 + k + (hh_ + 1) * HN], scalar=wcol(k),
                            in1=vo[:, hh_ * HN:(hh_ + 1) * HN], op0=ALU.mult, op1=ALU.add),
                            reads=[ures, (vres, hh_), "PV"], writes=[(vres, hh_)])
                S.emit("dve", lambda e: e.memset(S.fence_scratch, 0.0), reads=[(vres, 0), (vres, 1)], writes=[vres, "FSCR"])
            if th == 0:
                hs = (HALO, lambda kc: ("HHT", HHT[:, kc, :]))
            else:
                hs = (HALO, lambda kc, th=th: (("HT", kc, th - 1), HT[:, kc, th * TN - HALO: th * TN]))
            proj("wpw1_%d" % l, 0, 2 * DC, DC, [hs, hseg(th)], evac_pw1, [0, 1, 2, 3])
            _dbg("pw1_%d" % th)
            pm = PS[4][:, 0:TN]
            for cc in range(DC):
                S.emit("pe", lambda e, cc=cc: e.matmul(pm, lhsT=ONESD[:], rhs=VV[:, cc * TN:(cc + 1) * TN],
                                                        start=(cc == 0), stop=(cc == DC - 1)),
                       reads=[("VV", cc), "ONES"], writes=[("PS", 4)])
            mr, ma = tf()
            S.emit("act", lambda e, ma=ma: e.activation(out=ma, in_=pm, func=AF.Copy), reads=[("PS", 4)], writes=[mr])
            for cc in range(DC):
                vo = VV[:, cc * TN:(cc + 1) * TN]
                S.emit("dve", lambda e, vo=vo, ma=ma: e.tensor_tensor(out=vo, in0=vo, in1=ma, op=ALU.subtract),
                       reads=[("VV", cc), mr], writes=[("VV", cc)])
            pv_ = PS[5][:, 0:TN]
            for cc in range(DC):
                tr, ta = tf()
                S.emit("act", lambda e, ta=ta, cc=cc: e.activation(out=ta, in_=VV[:, cc * TN:(cc + 1) * TN], func=AF.Square),
                       reads=[("VV", cc)], writes=[tr])
                S.emit("pe", lambda e, ta=ta, cc=cc: e.matmul(pv_, lhsT=ONESD[:], rhs=ta, start=(cc == 0), stop=(cc == DC - 1)),
                       reads=[tr, "ONES"], writes=[("PS", 5)])
            sr, sa = tf()
            S.emit("act", lambda e, sa=sa: e.activation(out=sa, in_=pv_, func=AF.Sqrt, bias=EPST[:]),
                   reads=[("PS", 5), "EPST"], writes=[sr])
            rr, ra = tf()
            S.emit("dve", lambda e, ra=ra, sa=sa: e.reciprocal(out=ra, in_=sa), reads=[sr], writes=[rr])
            for cc in range(DC):
                vo = VV[:, cc * TN:(cc + 1) * TN]
                S.emit("dve", lambda e, vo=vo, ra=ra, cc=cc: e.scalar_tensor_tensor(
                    out=vo, in0=vo, scalar=pvc("lng_%d" % l, cc), in1=ra, op0=ALU.mult, op1=ALU.mult),
                    reads=[("VV", cc), rr, "PV"], writes=[("VV", cc)])
                S.emit("act", lambda e, vo=vo, cc=cc: e.activation(
                    out=YY[:, cc * TN:(cc + 1) * TN], in_=vo, func=AF.Silu, bias=pvc("lnb_%d" % l, cc)),
                    reads=[("VV", cc), "PV"], writes=[("YY", cc)])

            _dbg("ln_%d" % th)

            def evac_pw2(oc, outs, th=th):
                pr, pa = outs[0]
                xa = XT[:, oc, th * TN:(th + 1) * TN]
                S.emit("dve", lambda e, xa=xa, pa=pa, oc=oc: e.scalar_tensor_tensor(
                    out=xa, in0=pa, scalar=pvc("pw2b_%d" % l, oc), in1=xa, op0=ALU.add, op1=ALU.add),
                    reads=[pr, ("XT", oc, th), "PV"], writes=[("XT", oc, th)])
            proj("wpw2_%d" % l, 0, DC, DC, [(TN, lambda kc: (("YY", kc), YY[:, kc * TN:(kc + 1) * TN]))],
                 evac_pw2, [6, 7])
            _dbg("pw2_%d" % th)

    gvec_ready = [False]

    def build_gvec():
        S.fence("SCR")
        TABF = SMALL[0:33, 0:H]
        S.emit("dve", lambda e: e.memset(SMALL[0:64, 0:H], NEG), writes=["TAB"])
        S.emit("sp", lambda e: e.dma_start(out=SMALL[0:NBUCKET, 0:H], in_=relb_d), writes=["TAB"], dma="ld_tab")
        OHS = SCR[0:33, 0:2 * c.GW].bitcast(F32)
        S.emit("sp", lambda e: e.dma_start(out=OHS, in_=oh_d), writes=["OHS"], dma="ld_oh")
        GS = SCR[0:H, 2 * c.GW:3 * c.GW]
        for j0 in range(0, c.GW, 512):
            n = min(512, c.GW - j0)
            S.emit("pe", lambda e, j0=j0, n=n: e.matmul(PS[0][0:H, 0:n], lhsT=TABF, rhs=OHS[:, j0:j0 + n], start=True, stop=True),
                   reads=["TAB", "OHS"], writes=[("PS", 0)])
            S.emit("act", lambda e, j0=j0, n=n: e.activation(out=GS[:, j0:j0 + n], in_=PS[0][0:H, 0:n], func=AF.Copy),
                   reads=[("PS", 0)], writes=["GS"])
        for i in range(128):
            S.emit("sp", lambda e, i=i: e.dma_start(out=g2_d[:, i, :], in_=GS[:, 127 - i:127 - i + W2]),
                   reads=["GS"], writes=["gvec"], dma="st_gvec")
        S.fence("SCR")

    def attn_mixer(l, part="both"):
        NBLK, NBC, NQT, NKT = c.NBLK, c.NBC, c.NQT, c.NKT
        if part != "B":
            rmsnorm([xseg(th) for th in range(NTH)], "nm_%d" % l, [0, 1])
        QT = BIG[:, 0:H * T]
        MT = BIG[0:NBLK, H * T: H * T + 2 * T]
        KNS = BIG[:, H * T + 2 * T: H * T + 4 * T]
        VS = BIG[:, H * T + 4 * T: H * T + 6 * T]
        KM = SMALL[:, 32:32 + H * NBC]
        GQ = SMALL[:, 16:17]
        BTb = BIG[:, H * T + 6 * T: H * T + 8 * T]
        S.fence("BIG")
        if part == "B":
            S.emit("sp", lambda e: e.dma_start(out=QT, in_=qt_in),
                   writes=[("QT", h_, th_) for h_ in range(H) for th_ in range(NTH)], dma="ld_qt")
        S.emit("dve", lambda e: e.tensor_scalar(out=GQ, in0=pvc("qn_%d" % l, 0), scalar1=float(128 ** -0.5), scalar2=None,
                                                 op0=ALU.mult), reads=["PV"], writes=["GQ"])

        def evac_qkv(oc, outs):
            h, kind = oc // 3, oc % 3
            for th in range(NTH):
                pr, pa = outs[th]
                if kind == 2:
                    vr, va = tb()
                    S.emit("act", lambda e, va=va, pa=pa: e.activation(out=va, in_=pa, func=AF.Copy), reads=[pr], writes=[vr])
                    for tt in range(TN // 128):
                        qt = th * (TN // 128) + tt
                        b = 4 + (qt % 2)
                        ptb = PS[b][:].bitcast(BF16)[:, 0:128]
                        S.emit("pe", lambda e, ptb=ptb, va=va, tt=tt: e.transpose(ptb, va[:, tt * 128:(tt + 1) * 128], IDB[:]),
                               reads=[vr, "IDB"], writes=[("PS", b)])
                        vs = VS[:, (h % 2) * T + qt * 128:(h % 2) * T + (qt + 1) * 128]
                        S.emit("dve", lambda e, vs=vs, ptb=ptb: e.tensor_copy(out=vs, in_=ptb),
                               reads=[("PS", b)], writes=[("VS", h % 2)])
                    continue
                rr_, raw = tf()
                S.emit("act", lambda e, raw=raw, pa=pa: e.activation(out=raw, in_=pa, func=AF.Copy), reads=[pr], writes=[rr_])
                sr, sq = tf()
                S.emit("act", lambda e, sq=sq, pa=pa: e.activation(out=sq, in_=pa, func=AF.Square), reads=[pr], writes=[sr])
                b = 6 + (th % 2)
                S.emit("pe", lambda e, sq=sq, b=b: e.matmul(PS[b][:, 0:TN], lhsT=ONESH[:], rhs=sq, start=True, stop=True),
                       reads=[sr, "ONES"], writes=[("PS", b)])
                dr, sd = tf()
                S.emit("act", lambda e, sd=sd, b=b: e.activation(out=sd, in_=PS[b][:, 0:TN], func=AF.Sqrt, bias=EPST[:]),
                       reads=[("PS", b), "EPST"], writes=[dr])
                S.emit("dve", lambda e, sd=sd: e.reciprocal(out=sd, in_=sd), reads=[dr], writes=[dr])
                if kind == 0:
                    o = QT[:, h * T + th * TN: h * T + (th + 1) * TN]
                    S.emit("dve", lambda e, o=o, raw=raw, sd=sd: e.scalar_tensor_tensor(
                        out=o, in0=raw, scalar=GQ, in1=sd, op0=ALU.mult, op1=ALU.mult),
                        reads=[rr_, dr, "GQ"], writes=[("QT", h, th)])
                else:
                    o = KNS[:, (h % 2) * T + th * TN:(h % 2) * T + (th + 1) * TN]
                    S.emit("dve", lambda e, o=o, raw=raw, sd=sd: e.scalar_tensor_tensor(
                        out=o, in0=raw, scalar=pvc("kn_%d" % l, 0), in1=sd, op0=ALU.mult, op1=ALU.mult),
                        reads=[rr_, dr, "PV"], writes=[("KNS", h % 2)])
            if kind == 1:
                ks = KNS[:, (h % 2) * T:(h % 2 + 1) * T]
                S.emit("dve", lambda e, ks=ks, h=h: e.tensor_reduce(
                    out=KM[:, h * NBC:(h + 1) * NBC], in_=ks.rearrange("p (b k) -> p b k", k=BLK), axis=AX.X, op=ALU.add),
                    reads=[("KNS", h % 2)], writes=["KM"])
                S.emit("sp", lambda e, ks=ks, h=h: e.dma_start(out=kx_src[h * 128:(h + 1) * 128, :], in_=ks),
                       reads=[("KNS", h % 2)], writes=[("kx_src", h % 2)], dma="st_k%d" % (h % 2))
            if kind == 2:
                vs = VS[:, (h % 2) * T:(h % 2 + 1) * T]
                S.emit("sp", lambda e, vs=vs, h=h: e.dma_start(out=vx_src[h * 128:(h + 1) * 128, :], in_=vs),
                       reads=[("VS", h % 2)], writes=[("vx_src", h % 2)], dma="st_v%d" % (h % 2))
        _dbg("anorm")
        if part != "B":
            proj("wqkv_%d" % l, 0, 3 * H, DC, [hseg(th) for th in range(NTH)], evac_qkv, [0, 1, 2, 3])
            _dbg("qkv")
            S.emit("dve", lambda e: e.tensor_scalar(out=KM, in0=KM, scalar1=1.0 / BLK, scalar2=None, op0=ALU.mult),
                   reads=["KM"], writes=["KM"])
            S.emit("sp", lambda e: e.dma_start(out=km_src, in_=KM), reads=["KM"], writes=["km_src"], dma="st_km")
        if part == "A":
            S.emit("sp", lambda e: e.dma_start(out=qt_out, in_=QT),
                   reads=[("QT", h_, th_) for h_ in range(H) for th_ in range(NTH)], writes=["qt_out"], dma="st_qt")
            return
        if part == "both":
            for nm, src, dst in (("km", km_src, km_dst), ("kx", kx_src, kx_dst), ("vx", vx_src, vx_dst)):
                S.emit("pool", lambda e, src=src, dst=dst: e.collective_compute(
                    "AllGather", ALU.bypass, replica_groups=groups_b, ins=[src], outs=[dst]),
                    reads=[nm + "_src", (nm + "_src", 0), (nm + "_src", 1)], writes=[nm + "_dst"], dma="ag_%s_%d" % (nm, l), inc=1)
        S.fence("SCR")
        KMA = SCR[:, 0:2 * H * NBLK].bitcast(F32)
        KMB = SCR[:, 2 * H * NBLK: 3 * H * NBLK]
        for r in range(c.NB):
            dst = KMA.rearrange("p (h n) -> p h n", n=NBLK)[:, :, r * NBC:(r + 1) * NBC]
            S.emit("sp", lambda e, r=r, dst=dst: e.dma_start(
                out=dst, in_=km_dst[r * 128:(r + 1) * 128, :].rearrange("p (h n) -> p h n", n=NBC)),
                reads=["km_dst"], writes=["KMA"], dma="ld_kma")
        S.emit("dve", lambda e: e.tensor_copy(out=KMB, in_=KMA), reads=["KMA"], writes=["KMB"])
        _dbg("exch")
        o0 = 3 * H * NBLK
        o0 = (o0 + 63) // 64 * 64
        KTA = [SCR[:, o0 + i * c.S: o0 + (i + 1) * c.S] for i in range(2)]
        VTA = [SCR[:, o0 + (2 + i) * c.S: o0 + (3 + i) * c.S] for i in range(2)]
        BT = [BTb[:, i * T:(i + 1) * T] for i in range(2)]
        assert o0 + 4 * c.S <= SCRN, (o0 + 4 * c.S, SCRN)
        assert H * T + 8 * T <= BIGN
        bt_i = [0]
        for h in range(H):
            hb = h % 2
            for r in range(c.NB):
                S.emit("sp", lambda e, r=r, h=h, hb=hb: e.dma_start(
                    out=KTA[hb][:, r * T:(r + 1) * T], in_=kx_dst[(r * H + h) * 128:(r * H + h + 1) * 128, :]),
                    reads=["kx_dst"], writes=[("KTA", hb)], dma="ld_kta%d" % hb)
                S.emit("sp", lambda e, r=r, h=h, hb=hb: e.dma_start(
                    out=VTA[hb][:, r * T:(r + 1) * T], in_=vx_dst[(r * H + h) * 128:(r * H + h + 1) * 128, :]),
                    reads=["vx_dst"], writes=[("VTA", hb)], dma="ld_vta%d" % hb)
            mt = MT[:, hb * T:(hb + 1) * T]
            for qt in range(NQT):
                th, tq = qt // (TN // 128), qt % (TN // 128)
                qa = QT[:, h * T + qt * 128: h * T + (qt + 1) * 128]
                b = 6 + (qt % 2)
                S.emit("pe", lambda e, qa=qa, h=h, b=b: e.matmul(PS[b][:, 0:NBLK], lhsT=qa, rhs=KMB[:, h * NBLK:(h + 1) * NBLK],
                                                                 start=True, stop=True),
                       reads=[("QT", h, th), "KMB"], writes=[("PS", b)])
                gr, gt_ = tf()
                gm = gt_[:, 0:NBLK]
                m8 = gt_[:, 32:40]
                sel = gt_[:, 64:64 + NBLK]
                S.emit("dve", lambda e, gm=gm, b=b, qt=qt: e.tensor_tensor(
                    out=gm, in0=PS[b][:, 0:NBLK], in1=CV[:, c.cv_pastneg + qt * NBLK: c.cv_pastneg + (qt + 1) * NBLK], op=ALU.add),
                    reads=[("PS", b), "CV"], writes=[gr])
                S.emit("dve", lambda e, gm=gm, m8=m8: e.max(out=m8, in_=gm), reads=[gr], writes=[gr])
                S.emit("dve", lambda e, gm=gm, m8=m8, sel=sel: e.tensor_scalar(
                    out=sel, in0=gm, scalar1=m8[:, 2:3], scalar2=None, op0=ALU.is_ge), reads=[gr], writes=[gr])
                S.emit("dve", lambda e, sel=sel, qt=qt: e.tensor_tensor(
                    out=sel, in0=sel, in1=CV[:, c.cv_ownhot + qt * NBLK: c.cv_ownhot + (qt + 1) * NBLK], op=ALU.max),
                    reads=[gr, "CV"], writes=[gr])
                mr_, mb = tb()
                S.emit("dve", lambda e, sel=sel, mb=mb: e.tensor_scalar(
                    out=mb[:, 0:NBLK], in0=sel, scalar1=-1.0, scalar2=-NEG, op0=ALU.add, op1=ALU.mult),
                    reads=[gr], writes=[mr_])
                b2 = 4 + (qt % 2)
                ptb = PS[b2][:].bitcast(BF16)[0:NBLK, 0:128]
                S.emit("pe", lambda e, ptb=ptb, mb=mb: e.transpose(ptb, mb[:, 0:NBLK], IDB[:]),
                       reads=[mr_, "IDB"], writes=[("PS", b2)])
                S.emit("act", lambda e, ptb=ptb, mt=mt, qt=qt: e.activation(out=mt[:, qt * 128:(qt + 1) * 128], in_=ptb, func=AF.Copy),
                       reads=[("PS", b2)], writes=[("MT", hb)])
            _dbg("gate%d" % h)
            tiles = [(kt, qc) for kt in range(NKT) for qc in range(NTH)]
            SBANK = [0, 1, 6]
            bt_of, pts = {}, {}

            def stage1(i, h=h, hb=hb, mt=mt):
                kt, qc = tiles[i]
                if qc == 0:
                    bi = bt_i[0] % 2
                    bt_i[0] += 1
                    bt_of[kt] = bi
                    m0 = c.S - 128 - kt * 128
                    src = g2_d[h, :, m0:m0 + T]
                    S.emit("sp", lambda e, bi=bi, src=src: e.dma_start(out=BT[bi], in_=src),
                           reads=["gvec"], writes=[("BT", bi)], dma="ld_bt%d" % bi)
                bi = bt_of[kt]
                nblk = (kt * 128) // BLK
                sb_ = SBANK[i % 3]
                ps_s = PS[sb_][:, 0:TN]
                qa = QT[:, h * T + qc * TN: h * T + (qc + 1) * TN]
                S.emit("pe", lambda e: e.matmul(
                    ps_s, lhsT=KTA[hb][:, kt * 128:(kt + 1) * 128], rhs=qa, start=True, stop=False),
                    reads=[("KTA", hb), ("QT", h, qc)], writes=[("PS", sb_)])
                S.emit("pe", lambda e: e.matmul(
                    ps_s, lhsT=ESB[:, nblk * 128:(nblk + 1) * 128], rhs=mt[:, qc * TN:(qc + 1) * TN], start=False, stop=True),
                    reads=[("MT", hb), "ESB"], writes=[("PS", sb_)])
                tr, ta = tf()
                S.emit("dve", lambda e: e.tensor_tensor(
                    out=ta, in0=ps_s, in1=BT[bi][:, qc * TN:(qc + 1) * TN], op=ALU.add),
                    reads=[("PS", sb_), ("BT", bi)], writes=[tr])
                pr_, pt = tb()
                S.emit("act", lambda e: e.activation(out=pt, in_=ta, func=AF.Exp), reads=[tr], writes=[pr_])
                pts[i] = (pr_, pt)

            def stage2(i, hb=hb):
                kt, qc = tiles[i]
                pr_, pt = pts.pop(i)
                S.emit("pe", lambda e: e.matmul(
                    PS[2 + qc][:, 0:TN], lhsT=VTA[hb][:, kt * 128:(kt + 1) * 128], rhs=pt, start=(kt == 0), stop=(kt == NKT - 1)),
                    reads=[("VTA", hb), pr_], writes=[("PS", 2 + qc)])
                S.emit("pe", lambda e: e.matmul(
                    PS[4 + qc][:, 0:TN], lhsT=ONESB[:], rhs=pt, start=(kt == 0), stop=(kt == NKT - 1)),
                    reads=[pr_, "ONES"], writes=[("PS", 4 + qc)])
            DEPTH = 2
            for i in range(len(tiles) + DEPTH):
                if i < len(tiles):
                    stage1(i)
                if i - DEPTH >= 0:
                    stage2(i - DEPTH)
            _dbg("core%d" % h)
            for qc in range(NTH):
                rr, ra = tf()
                S.emit("dve", lambda e, ra=ra, qc=qc: e.reciprocal(out=ra, in_=PS[4 + qc][:, 0:TN]),
                       reads=[("PS", 4 + qc)], writes=[rr])
                S.emit("dve", lambda e, ra=ra, qc=qc, h=h: e.tensor_tensor(
                    out=HT[:, h, qc * TN:(qc + 1) * TN], in0=PS[2 + qc][:, 0:TN], in1=ra, op=ALU.mult),
                    reads=[("PS", 2 + qc), rr], writes=[("HT", h, qc)])

        _dbg("heads")
        S.fence("SCR")

        def evac_o(oc, outs):
            for th in range(NTH):
                pr, pa = outs[th]
                xa = XT[:, oc, th * TN:(th + 1) * TN]
                S.emit("dve", lambda e, xa=xa, pa=pa: e.tensor_tensor(out=xa, in0=xa, in1=pa, op=ALU.add),
                       reads=[pr, ("XT", oc, th)], writes=[("XT", oc, th)])
        proj("wo_%d" % l, 0, DC, DC, [hseg(th) for th in range(NTH)], evac_o, [0, 1, 6, 7])

    phases = []
    if unf:
        phases = list(steps)
        if "attnB" in kinds:
            build_gvec()
    else:
        for l in range(L):
            phases.append(("ffn1", l))
            phases.append(("mix", l))
            phases.append(("ffn2", l))
        if L > 1:
            build_gvec()
    try:
      for kind, l in phases:
        if kind == "ffn1":
            ffn("1", l)
        elif kind == "ffn2":
            ffn("2", l)
        elif kind == "conv":
            conv_mixer(l)
        elif kind == "attnA":
            attn_mixer(l, "A")
        elif kind == "attnB":
            attn_mixer(l, "B")
        elif l % 2 == 0:
            conv_mixer(l)
        else:
            attn_mixer(l)
        if stop_after == (kind, l):
            break
    except _Stop:
        pass

    for cc in range(DC):
        S.emit("sp", lambda e, cc=cc: e.dma_start(out=yT_d[cc * 128:(cc + 1) * 128, :], in_=XT[:, cc, :]),
               reads=[("XT", cc, th) for th in range(NTH)], writes=["yT"], dma="st_y")
    with nc.Block() as block:
        S.finalize(block)
    es.close()
    return nc


def run(cfg, inputs, ag_weights=True, n_layers=None, stop_after=None, trace=False):
    shared, per_core = host_prepare(cfg, inputs)
    nc = build_program(cfg, ag_weights=ag_weights, n_layers=n_layers, stop_after=stop_after)
    L = cfg.L if n_layers is None else n_layers
    wnames = [s[0] for s in cfg.weight_specs() if int(s[0].split("_")[1]) < L]
    in_maps = []
    for core in range(NCORES):
        m = dict(per_core[core])
        for k in ("pvec", "relb", "ident", "esel"):
            m[k] = shared[k]
        for name in wnames:
            w = shared[name]
            if ag_weights:
                r = w.shape[0] // NCORES
                m[name] = w[core * r:(core + 1) * r]
            else:
                m[name] = w
        in_maps.append(m)
    res = run_bass_kernel_spmd(nc, in_maps, core_ids=list(range(NCORES)), trace=trace)
    B = NCORES // cfg.NB
    out = np.zeros((B, cfg.S, cfg.D), np.float32)
    for core in range(NCORES):
        b, cl = core // cfg.NB, core % cfg.NB
        out[b, cl * cfg.T:(cl + 1) * cfg.T, :] = res.results[core]["yT"].T
    return out, res


LAUNCHES = [
    [("ffn1", 0)],
    [("conv", 0), ("ffn2", 0), ("ffn1", 1), ("attnA", 1)],
    [("attnB", 1), ("ffn2", 1), ("ffn1", 2)],
    [("conv", 2), ("ffn2", 2), ("ffn1", 3), ("attnA", 3)],
    [("attnB", 3), ("ffn2", 3)],
]


def run_unfused(cfg, inputs, launches=None):
    c = cfg
    launches = LAUNCHES if launches is None else launches
    shared, per_core = host_prepare(cfg, inputs)
    xT = [per_core[i]["xT"] for i in range(NCORES)]
    carry = None
    for steps in launches:
        nc = build_program(cfg, steps=steps)
        kinds = set(k for k, _ in steps)
        need = set()
        for kind, l in steps:
            need |= {"ffn1": {"wgu1_%d" % l, "wd1_%d" % l}, "ffn2": {"wgu2_%d" % l, "wd2_%d" % l},
                     "conv": {"wpw1_%d" % l, "wpw2_%d" % l}, "attnA": {"wqkv_%d" % l}, "attnB": {"wo_%d" % l}}[kind]
        in_maps = []
        for core in range(NCORES):
            b = core // c.NB
            ranks = list(range(b * c.NB, (b + 1) * c.NB))
            m = {"xT": xT[core], "cvec": per_core[core]["cvec"], "oh": per_core[core]["oh"]}
            for k in ("pvec", "relb", "ident", "esel"):
                m[k] = shared[k]
            for name in need:
                m[name] = shared[name]
            if "conv" in kinds:
                m["halo_dst"] = np.concatenate([xT[r][:, c.T - HALO:] for r in ranks], axis=0)
            if "attnB" in kinds:
                m["qt_in"] = carry[core]["qt_out"]
                for nm in ("kx", "vx", "km"):
                    m[nm + "_dst"] = np.concatenate([carry[r][nm + "_src"] for r in ranks], axis=0)
            in_maps.append(m)
        res = run_bass_kernel_spmd(nc, in_maps, core_ids=list(range(NCORES)))
        xT = [res.results[i]["yT"] for i in range(NCORES)]
        carry = res.results
    B = NCORES // c.NB
    out = np.zeros((B, c.S, c.D), np.float32)
    for core in range(NCORES):
        b, cl = core // c.NB, core % c.NB
        out[b, cl * c.T:(cl + 1) * c.T, :] = xT[core].T
    return out


def kernel(**inputs):
    cfg = Cfg(**FULL_CFG)
    return run_unfused(cfg, inputs)
```

```python
import numpy as np
from contextlib import ExitStack
import concourse.bass as bass
import concourse.mybir as mybir
from concourse.bass_utils import run_bass_kernel_spmd

F32 = mybir.dt.float32
BF16 = mybir.dt.bfloat16
ALU = mybir.AluOpType
AF = mybir.ActivationFunctionType
AX = mybir.AxisListType

NCORES = 8
CONV_W = 31
HALO = 32
BLK = 256
NBUCKET = 32
EPS = 1e-6
NEG = -1.0e4

FULL_CFG = dict(D=2048, F=5632, T=1024, S=4096, L=4)
DEBUG_STOP = None
RELAX_SAME_ENGINE = True
import os
ALL_READERS = os.environ.get('MK_ALLR', '0') == '1'


class _Stop(Exception):
    pass


def _dbg(tag):
    if DEBUG_STOP == tag:
        raise _Stop()


class Sched:
    ENGS = ("pe", "act", "dve", "pool", "sp")

    def __init__(self, nc, es):
        self.nc = nc
        self.es = es
        self.insts = []
        self.last_w = {}
        self.readers = {}
        self.dma_sems = {}
        self.arena_tags = {}
        self.eng_sems = {e: es.enter_context(nc.semaphore("sem_" + e)) for e in self.ENGS}

    def _dma_sem(self, key):
        if key not in self.dma_sems:
            self.dma_sems[key] = [self.es.enter_context(self.nc.semaphore("dsem_%d" % len(self.dma_sems))), 0]
        return self.dma_sems[key]

    def emit(self, eng, fn, reads=(), writes=(), dma=None, inc=16):
        i = len(self.insts)
        deps = set()
        reads = list(reads)
        for k in list(reads) + list(writes):
            a = self.arena_tags.get(k[0] if isinstance(k, tuple) else k)
            if a is not None and ("EPOCH", a) not in reads and ("EPOCH", a) not in writes:
                reads.append(("EPOCH", a))
        for r in reads:
            if r in self.last_w:
                deps.add(self.last_w[r])
        for w in writes:
            if w in self.last_w:
                deps.add(self.last_w[w])
            for rd in self.readers.get(w, {}).values():
                deps.add(rd)
        deps.discard(i)
        rec = dict(eng=eng, fn=fn, deps=deps, dma=dma, inc=inc, token=None)
        if dma is not None:
            s = self._dma_sem(dma)
            s[1] += inc
            rec["token"] = (s[0], s[1])
        self.insts.append(rec)
        rkey = eng if dma is None else ("dma", dma)
        if ALL_READERS:
            rkey = i
        for r in reads:
            self.readers.setdefault(r, {})[rkey] = i
        for w in writes:
            self.last_w[w] = i
            self.readers[w] = {}
        return i

    def fence(self, arena):
        fs = self.fence_scratch
        self.emit("dve", lambda e: e.memset(fs, 0.0), writes=[("EPOCH", arena), "FSCR"])

    def finalize(self, block):
        insts = self.insts
        needed = set()
        pos = {}
        npos = {e: 0 for e in self.ENGS}
        for j, rec in enumerate(insts):
            pos[j] = npos[rec["eng"]]
            npos[rec["eng"]] += 1
        for j, rec in enumerate(insts):
            keep = set()
            for i in rec["deps"]:
                src = insts[i]
                if src["dma"] is None and src["eng"] == rec["eng"] and rec["eng"] == "pe" and rec["dma"] is None:
                    continue
                if RELAX_SAME_ENGINE and src["dma"] is None and rec["dma"] is None and src["eng"] == rec["eng"] and pos[j] - pos[i] >= 2:
                    continue
                keep.add(i)
                if src["dma"] is None:
                    needed.add(i)
            rec["deps"] = keep
        cnt = {e: 0 for e in self.ENGS}
        for i, rec in enumerate(insts):
            if rec["dma"] is None and i in needed:
                cnt[rec["eng"]] += 1
                rec["token"] = (self.eng_sems[rec["eng"]], cnt[rec["eng"]])
                rec["signal"] = True
            else:
                rec["signal"] = rec["dma"] is not None
        progs = {e: [] for e in self.ENGS}
        waited = {e: {} for e in self.ENGS}
        for rec in insts:
            e = rec["eng"]
            need = {}
            for i in rec["deps"]:
                sem, val = insts[i]["token"]
                k = id(sem)
                if waited[e].get(k, 0) >= val:
                    continue
                if k not in need or need[k][1] < val:
                    need[k] = (sem, val)
            for k, (sem, val) in need.items():
                waited[e][k] = val
            progs[e].append((list(need.values()), rec))

        def runner(items):
            def run(eng):
                for waits, rec in items:
                    for sem, val in waits:
                        eng.wait_ge(sem, val)
                    ins = rec["fn"](eng)
                    if rec["signal"]:
                        if rec["dma"] is not None:
                            if rec["inc"] == 1:
                                ins.then_inc(rec["token"][0])
                            else:
                                ins.then_inc(rec["token"][0], rec["inc"])
                        else:
                            ins.then_inc(rec["token"][0], 1)
            return run

        fin = []
        for key, (sem, val) in self.dma_sems.items():
            fin.append((sem, val))
        sp_items = progs["sp"]

        def sp_run(eng):
            runner(sp_items)(eng)
            for e2 in self.ENGS:
                if cnt[e2] > 0:
                    eng.wait_ge(self.eng_sems[e2], cnt[e2])
            for sem, val in fin:
                eng.wait_ge(sem, val)

        block.tensor(runner(progs["pe"]))
        block.scalar(runner(progs["act"]))
        block.vector(runner(progs["dve"]))
        block.gpsimd(runner(progs["pool"]))
        block.sync(sp_run)


def lay_w(w):
    K, N = w.shape
    kc, oc = K // 128, N // 128
    t = w.reshape(kc, 128, oc, 128)
    t = np.ascontiguousarray(t.transpose(2, 1, 0, 3))
    return t.reshape(oc * 128, kc * 128)


def interleave_cols(a, b):
    K, N = a.shape
    oc = N // 128
    t = np.stack([a.reshape(K, oc, 128), b.reshape(K, oc, 128)], axis=2)
    return t.reshape(K, 2 * N)


def fm(v):
    return np.ascontiguousarray(v.reshape(-1, 128).T)


def t5_bucket_np(d):
    max_exact = NBUCKET // 2
    n = np.maximum(d, 0)
    nf = np.maximum(n, 1).astype(np.float32)
    large = max_exact + (np.log(nf / np.float32(max_exact)) / np.float32(np.log(2048 / max_exact))
                         * np.float32(NBUCKET - max_exact)).astype(np.int32)
    large = np.minimum(large, NBUCKET - 1)
    return np.where(n < max_exact, n, large)


class Cfg:
    def __init__(self, D, F, T, S, L):
        self.D, self.F, self.T, self.S, self.L = D, F, T, S, L
        self.DC = D // 128
        self.FC = F // 128
        self.FCH = self.FC // 2
        self.H = D // 128
        self.NB = S // T
        self.NG = NCORES // self.NB
        self.TN = min(512, T)
        self.NTH = T // self.TN
        self.NBLK = S // BLK
        self.NBC = T // BLK
        self.NQT = T // 128
        self.NKT = S // 128
        self.GW = S + T
        off = 0
        self.pv = {}

        def add(name, n):
            nonlocal off
            self.pv[name] = off
            off += n
        for l in range(L):
            add("n1_%d" % l, self.DC)
            add("nm_%d" % l, self.DC)
            add("n2_%d" % l, self.DC)
            if l % 2 == 0:
                add("pw1b_%d" % l, 2 * self.DC)
                add("dww_%d" % l, CONV_W * self.DC)
                add("dwb_%d" % l, self.DC)
                add("lng_%d" % l, self.DC)
                add("lnb_%d" % l, self.DC)
                add("pw2b_%d" % l, self.DC)
            else:
                add("qn_%d" % l, 1)
                add("kn_%d" % l, 1)
        self.NP = off
        self.cv_hasprev = 0
        self.cv_prevsel = 1
        self.cv_pastneg = 1 + self.NB
        self.cv_ownhot = self.cv_pastneg + self.NQT * self.NBLK
        self.NCV = self.cv_ownhot + self.NQT * self.NBLK

    def weight_specs(self):
        sp = []
        for l in range(self.L):
            sp.append(("wgu1_%d" % l, 2 * self.F, self.D))
            sp.append(("wd1_%d" % l, 2 * self.D, self.F // 2))
            if l % 2 == 0:
                sp.append(("wpw1_%d" % l, 2 * self.D, self.D))
                sp.append(("wpw2_%d" % l, self.D, self.D))
            else:
                sp.append(("wqkv_%d" % l, 3 * self.D, self.D))
                sp.append(("wo_%d" % l, self.D, self.D))
            sp.append(("wgu2_%d" % l, 2 * self.F, self.D))
            sp.append(("wd2_%d" % l, 2 * self.D, self.F // 2))
        return sp


def host_prepare(cfg, inputs):
    c = cfg
    g = {k: np.asarray(v, dtype=np.float32) for k, v in inputs.items()}
    shared = {}
    for l in range(c.L):
        for tag in ("1", "2"):
            wg, wu, wd = g["ffn%s_w_gate" % tag][l], g["ffn%s_w_up" % tag][l], g["ffn%s_w_down" % tag][l]
            shared["wgu%s_%d" % (tag, l)] = lay_w(interleave_cols(wg, wu))
            hf = c.F // 2
            shared["wd%s_%d" % (tag, l)] = np.concatenate([lay_w(wd[:hf]), lay_w(wd[hf:])], axis=0)
        j = l // 2
        if l % 2 == 0:
            w1 = g["conv_pw1_w"][j]
            shared["wpw1_%d" % l] = lay_w(interleave_cols(w1[:, :c.D], w1[:, c.D:]))
            shared["wpw2_%d" % l] = lay_w(g["conv_pw2_w"][j])
        else:
            wq = g["attn_wqkv"][j]
            K = wq.shape[0]
            t = np.stack([wq[:, i * c.D:(i + 1) * c.D].reshape(K, c.H, 128) for i in range(3)], axis=2)
            shared["wqkv_%d" % l] = lay_w(t.reshape(K, 3 * c.D))
            shared["wo_%d" % l] = lay_w(g["attn_wo"][j])
    pvec = np.zeros((128, c.NP), np.float32)

    def put(name, arr):
        o = c.pv[name]
        pvec[:, o:o + arr.shape[1]] = arr
    for l in range(c.L):
        put("n1_%d" % l, fm(g["ffn1_norm"][l]))
        put("nm_%d" % l, fm(g["mix_norm"][l]))
        put("n2_%d" % l, fm(g["ffn2_norm"][l]))
        j = l // 2
        if l % 2 == 0:
            b1 = g["conv_pw1_b"][j]
            ba, bg = fm(b1[:c.D]), fm(b1[c.D:])
            put("pw1b_%d" % l, np.stack([ba, bg], axis=2).reshape(128, 2 * c.DC))
            dw = g["conv_dw_w"][j]
            put("dww_%d" % l, np.ascontiguousarray(dw.reshape(CONV_W, c.DC, 128).transpose(2, 0, 1)).reshape(128, CONV_W * c.DC))
            put("dwb_%d" % l, fm(g["conv_dw_b"][j]))
            put("lng_%d" % l, fm(g["conv_ln_g"][j]))
            put("lnb_%d" % l, fm(g["conv_ln_b"][j]))
            put("pw2b_%d" % l, fm(g["conv_pw2_b"][j]))
        else:
            put("qn_%d" % l, g["attn_q_norm"][j].reshape(128, 1))
            put("kn_%d" % l, g["attn_k_norm"][j].reshape(128, 1))
    shared["pvec"] = pvec
    shared["relb"] = g["rel_bias"]
    shared["ident"] = np.eye(128, dtype=np.float32)
    es = np.zeros((c.NBLK, c.NBLK, 128), np.float32)
    for n in range(c.NBLK):
        es[n, n, :] = 1.0
    shared["esel"] = es.reshape(c.NBLK, c.NBLK * 128)
    x = g["x"]
    per_core = []
    for core in range(NCORES):
        b, cl = core // c.NB, core % c.NB
        d = {}
        d["xT"] = np.ascontiguousarray(x[b, cl * c.T:(cl + 1) * c.T, :].T)
        cv = np.zeros((128, c.NCV), np.float32)
        cv[:, c.cv_hasprev] = 1.0 if cl > 0 else 0.0
        if cl > 0:
            cv[:, c.cv_prevsel + cl - 1] = 1.0
        for qt in range(c.NQT):
            own = cl * c.NBC + (qt * 128) // BLK
            for n in range(c.NBLK):
                cv[:, c.cv_pastneg + qt * c.NBLK + n] = 0.0 if n < own else -3.0e38
                cv[:, c.cv_ownhot + qt * c.NBLK + n] = 1.0 if n == own else 0.0
        d["cvec"] = cv
        jj = np.arange(c.GW)
        dist = jj - (c.S - 1) + cl * c.T
        oh = np.zeros((33, c.GW), np.float32)
        bk = t5_bucket_np(dist.astype(np.int64))
        oh[bk[dist >= 0], jj[dist >= 0]] = 1.0
        oh[32, jj[dist < 0]] = 1.0
        d["oh"] = oh
        per_core.append(d)
    return shared, per_core


def build_program(cfg, ag_weights=True, n_layers=None, stop_after=None, steps=None):
    c = cfg
    L = c.L if n_layers is None else n_layers
    D, F, T, DC, FC, FCH, H, TN, NTH = c.D, c.F, c.T, c.DC, c.FC, c.FCH, c.H, c.TN, c.NTH
    nc = bass.Bass("TRN2", target_bir_lowering=False)
    es = ExitStack()
    S = Sched(nc, es)
    groups_all = [list(range(NCORES))]
    groups_b = [list(range(b * c.NB, (b + 1) * c.NB)) for b in range(c.NG)]

    xT_d = nc.dram_tensor("xT", [D, T], F32, kind="ExternalInput").ap()
    yT_d = nc.dram_tensor("yT", [D, T], F32, kind="ExternalOutput").ap()
    pvec_d = nc.dram_tensor("pvec", [128, c.NP], F32, kind="ExternalInput").ap()
    cvec_d = nc.dram_tensor("cvec", [128, c.NCV], F32, kind="ExternalInput").ap()
    relb_d = nc.dram_tensor("relb", [NBUCKET, H], F32, kind="ExternalInput").ap()
    ident_d = nc.dram_tensor("ident", [128, 128], F32, kind="ExternalInput").ap()
    esel_d = nc.dram_tensor("esel", [c.NBLK, c.NBLK * 128], F32, kind="ExternalInput").ap()
    oh_d = nc.dram_tensor("oh", [33, c.GW], F32, kind="ExternalInput").ap()
    unf = steps is not None
    if unf:
        ag_weights = False
        need = set()
        for kind, l in steps:
            if kind == "ffn1":
                need |= {"wgu1_%d" % l, "wd1_%d" % l}
            elif kind == "ffn2":
                need |= {"wgu2_%d" % l, "wd2_%d" % l}
            elif kind == "conv":
                need |= {"wpw1_%d" % l, "wpw2_%d" % l}
            elif kind == "attnA":
                need |= {"wqkv_%d" % l}
            elif kind == "attnB":
                need |= {"wo_%d" % l}
        wspecs = [s for s in c.weight_specs() if s[0] in need]
        kinds = set(k for k, _ in steps)
    else:
        wspecs = [s for s in c.weight_specs() if int(s[0].split("_")[1]) < L]
        kinds = set()
    w_in, w_full = {}, {}
    for name, rows, cols in wspecs:
        if ag_weights:
            w_in[name] = nc.dram_tensor(name, [rows // NCORES, cols], F32, kind="ExternalInput").ap()
            w_full[name] = nc.dram_tensor(name + "_f", [rows, cols], F32).ap()
        else:
            w_full[name] = nc.dram_tensor(name, [rows, cols], F32, kind="ExternalInput").ap()
    W2 = c.GW - 127
    g2_d = nc.dram_tensor("gvec2", [H, 128, W2], BF16).ap()
    def xt(name, shape, dt, ext_in=False, ext_out=False):
        if ext_in:
            return nc.dram_tensor(name, shape, dt, kind="ExternalInput").ap()
        if ext_out:
            return nc.dram_tensor(name, shape, dt, kind="ExternalOutput").ap()
        return nc.dram_tensor(name, shape, dt).ap()
    halo_src = xt("halo_src", [D, HALO], F32)
    halo_dst = xt("halo_dst", [c.NB * D, HALO], F32, ext_in=(unf and "conv" in kinds))
    kx_src = xt("kx_src", [H * 128, T], BF16, ext_out=(unf and "attnA" in kinds))
    vx_src = xt("vx_src", [H * 128, T], BF16, ext_out=(unf and "attnA" in kinds))
    km_src = xt("km_src", [128, H * c.NBC], F32, ext_out=(unf and "attnA" in kinds))
    kx_dst = xt("kx_dst", [c.NB * H * 128, T], BF16, ext_in=(unf and "attnB" in kinds))
    vx_dst = xt("vx_dst", [c.NB * H * 128, T], BF16, ext_in=(unf and "attnB" in kinds))
    km_dst = xt("km_dst", [c.NB * 128, H * c.NBC], F32, ext_in=(unf and "attnB" in kinds))
    qt_out = xt("qt_out", [128, H * T], BF16, ext_out=True) if (unf and "attnA" in kinds) else None
    qt_in = xt("qt_in", [128, H * T], BF16, ext_in=True) if (unf and "attnB" in kinds) else None

    def sb(name, shape, dt):
        return es.enter_context(nc.sbuf_tensor(name, shape, dt))
    XT = sb("XT", [128, DC, T], F32)
    HT = sb("HT", [128, DC, T], BF16)
    UW = HALO + TN
    BIGN = max(FCH * T, 3 * DC * TN + 2 * UW, H * T + 6 * T + 2 * (T + 3 * 128))
    BIG = sb("BIG", [128, BIGN], BF16)
    WMAX = max(FCH, DC) * 128
    NWB = 6
    SCRN = max(NWB * WMAX, 3 * H * c.NBLK + 64 + 4 * c.S, 3 * c.GW)
    SCR = sb("SCR", [128, SCRN], BF16)
    PV = sb("PV", [128, c.NP], F32)
    CV = sb("CV", [128, c.NCV], F32)
    NTF, NTB = 4, 4
    TF = sb("TF", [128, NTF, TN], F32)
    TB = sb("TB", [128, NTB, TN], BF16)
    ONESD = sb("ONESD", [128, 128], F32)
    ONESH = sb("ONESH", [128, 128], F32)
    ONESB = sb("ONESB", [128, 128], BF16)
    IDB = sb("IDB", [128, 128], BF16)
    ESB = sb("ESB", [c.NBLK, c.NBLK * 128], BF16)
    EPST = sb("EPST", [128, 1], F32)
    HHT = sb("HHT", [128, DC, HALO], BF16)
    XH = sb("XH", [128, DC, HALO], F32)
    XHR = HT[:].rearrange("p c t -> p (c t)")[:, 0:2 * c.NB * DC * HALO].bitcast(F32).rearrange("p (r k) -> p r k", r=c.NB)
    SMALL = sb("SMALL", [128, 128], F32)
    S.fence_scratch = SMALL[:, 127:128]
    ALLHT = [("HT", cc, th) for cc in range(DC) for th in range(NTH)]
    BIG_TAGS = ("HID", "VV", "YY", "UE", "QT", "MT", "KNS", "VS", "BT")
    SCR_TAGS = ("W", "OHS", "GS", "KMA", "KMB", "KTA", "VTA")
    for t_ in BIG_TAGS:
        S.arena_tags[t_] = "BIG"
    for t_ in SCR_TAGS:
        S.arena_tags[t_] = "SCR"
    PS = [es.enter_context(nc.psum_tensor("ps%d" % i, [128, 512], F32)) for i in range(8)]

    tf_i = [0]
    tb_i = [0]

    def tf():
        i = tf_i[0] % NTF
        tf_i[0] += 1
        return ("TF", i), TF[:, i, :]

    def tb():
        i = tb_i[0] % NTB
        tb_i[0] += 1
        return ("TB", i), TB[:, i, :]

    def pvc(name, col, n=1):
        o = c.pv[name] + col
        return PV[:, o:o + n]

    S.emit("sp", lambda e: e.dma_start(out=PV[:], in_=pvec_d), writes=["PV"], dma="ld_pv")
    S.emit("sp", lambda e: e.dma_start(out=CV[:], in_=cvec_d), writes=["CV"], dma="ld_cv")
    S.emit("pool", lambda e: e.dma_start(out=IDB[:], in_=ident_d), writes=["IDB"], dma="ld_id")
    S.emit("pool", lambda e: e.dma_start(out=ESB[:], in_=esel_d), writes=["ESB"], dma="ld_es")
    ALLXT = [("XT", cc, th) for cc in range(DC) for th in range(NTH)]
    for cc in range(DC):
        S.emit("sp", lambda e, cc=cc: e.dma_start(out=XT[:, cc, :], in_=xT_d[cc * 128:(cc + 1) * 128, :]),
               writes=ALLXT, dma="ld_x")
    S.emit("dve", lambda e: e.memset(ONESD[:], 1.0 / D), writes=["ONES"])
    S.emit("dve", lambda e: e.memset(ONESH[:], 1.0 / 128), writes=["ONES"])
    S.emit("dve", lambda e: e.memset(ONESB[:], 1.0), writes=["ONES"])
    S.emit("dve", lambda e: e.memset(EPST[:], EPS), writes=["EPST"])

    wsrcs = {}
    if ag_weights:
        for name, rows, cols in wspecs:
            src = nc.dram_tensor(name + "_s", [rows // NCORES, cols], F32).ap()
            wsrcs[name] = src
            S.emit("pool", lambda e, src=src, name=name: e.dma_start(out=src, in_=w_in[name]),
                   writes=["wsrc_all"], dma="wcp")
        for name, rows, cols in wspecs:
            S.emit("pool", lambda e, name=name: e.collective_compute(
                "AllGather", ALU.bypass, replica_groups=groups_all, ins=[wsrcs[name]], outs=[w_full[name]]),
                reads=["wsrc_all"], writes=[("wfull", name)], dma="wag_" + name, inc=1)

    wslot = [0]

    def proj(wname, row0, OC, KC, segs, evac, ps_banks, defer=0):
        wd = w_full[wname]
        nb = len(ps_banks)
        bi = 0
        pend = []
        for oc in range(OC):
            slot = wslot[0] % NWB
            wslot[0] += 1
            wt = SCR[:, slot * WMAX: slot * WMAX + KC * 128]
            S.emit("pool", lambda e, wt=wt, oc=oc: e.dma_start(
                out=wt, in_=wd[row0 + oc * 128: row0 + (oc + 1) * 128, 0:KC * 128]),
                reads=[("wfull", wname)], writes=[("W", slot)], dma="w%d" % slot)
            outs = []
            for (n, fn) in segs:
                b = ps_banks[bi % nb]
                bi += 1
                outs.append((("PS", b), PS[b][:, 0:n]))
            for kc in range(KC):
                for si, (n, fn) in enumerate(segs):
                    res, ap = fn(kc)
                    S.emit("pe", lambda e, o=outs[si][1], wt=wt, kc=kc, ap=ap, KC=KC: e.matmul(
                        o, lhsT=wt[:, kc * 128:(kc + 1) * 128], rhs=ap, start=(kc == 0), stop=(kc == KC - 1)),
                        reads=[("W", slot), res], writes=[outs[si][0]])
            pend.append((oc, outs))
            if len(pend) > defer:
                evac(*pend.pop(0))
        while pend:
            evac(*pend.pop(0))

    def rmsnorm(segs, gname, ps_banks, ones=None, nch=None):
        ones = ONESD if ones is None else ones
        nch = DC if nch is None else nch
        for si, (n, xin, hout) in enumerate(segs):
            b = ps_banks[si % len(ps_banks)]
            pst = PS[b][:, 0:n]
            for cc in range(nch):
                xr, xa = xin(cc)
                tr, ta = tf()
                S.emit("act", lambda e, ta=ta, xa=xa, n=n: e.activation(out=ta[:, 0:n], in_=xa, func=AF.Square),
                       reads=[xr], writes=[tr])
                S.emit("pe", lambda e, pst=pst, ta=ta, n=n, cc=cc: e.matmul(
                    pst, lhsT=ones[:], rhs=ta[:, 0:n], start=(cc == 0), stop=(cc == nch - 1)),
                    reads=[tr, "ONES"], writes=[("PS", b)])
            sr, sa = tf()
            S.emit("act", lambda e, sa=sa, pst=pst, n=n: e.activation(out=sa[:, 0:n], in_=pst, func=AF.Sqrt, bias=EPST[:]),
                   reads=[("PS", b), "EPST"], writes=[sr])
            rr, ra = tf()
            S.emit("dve", lambda e, ra=ra, sa=sa, n=n: e.reciprocal(out=ra[:, 0:n], in_=sa[:, 0:n]),
                   reads=[sr], writes=[rr])
            for cc in range(nch):
                xr, xa = xin(cc)
                hr, ha = hout(cc)
                S.emit("dve", lambda e, ha=ha, xa=xa, ra=ra, cc=cc, n=n: e.scalar_tensor_tensor(
                    out=ha, in0=xa, scalar=pvc(gname, cc), in1=ra[:, 0:n], op0=ALU.mult, op1=ALU.mult),
                    reads=[xr, rr, "PV"], writes=[hr])

    def xseg(th):
        return (TN,
                lambda cc, th=th: (("XT", cc, th), XT[:, cc, th * TN:(th + 1) * TN]),
                lambda cc, th=th: (("HT", cc, th), HT[:, cc, th * TN:(th + 1) * TN]))

    def hseg(th):
        return (TN, lambda kc, th=th: (("HT", kc, th), HT[:, kc, th * TN:(th + 1) * TN]))

    def ffn(tag, l):
        S.fence("BIG")
        _dbg("load")
        rmsnorm([xseg(th) for th in range(NTH)], "n%s_%d" % (tag, l), [0, 1])
        _dbg("norm")
        HID = BIG[:, 0:FCH * T]
        for hf in range(2):
            def evac_gu(oc, outs, hf=hf):
                if oc % 2 == 0:
                    evac_gu.gate = outs
                    return
                fcl = oc // 2
                for th in range(NTH):
                    (gr, ga), (ur, ua) = evac_gu.gate[th], outs[th]
                    tr, ta = tf()
                    S.emit("act", lambda e, ta=ta, ga=ga: e.activation(out=ta, in_=ga, func=AF.Silu),
                           reads=[gr], writes=[tr])
                    o = HID[:, fcl * T + th * TN: fcl * T + (th + 1) * TN]
                    S.emit("dve", lambda e, o=o, ta=ta, ua=ua: e.tensor_tensor(out=o, in0=ta, in1=ua, op=ALU.mult),
                           reads=[tr, ur], writes=[("HID", fcl, th)])
            proj("wgu%s_%d" % (tag, l), hf * FCH * 256, 2 * FCH, DC, [hseg(th) for th in range(NTH)], evac_gu,
                 list(range(8)))

            def evac_d(oc, outs):
                for th in range(NTH):
                    pr, pa = outs[th]
                    xa = XT[:, oc, th * TN:(th + 1) * TN]
                    S.emit("dve", lambda e, xa=xa, pa=pa: e.scalar_tensor_tensor(
                        out=xa, in0=pa, scalar=0.5, in1=xa, op0=ALU.mult, op1=ALU.add),
                        reads=[pr, ("XT", oc, th)], writes=[("XT", oc, th)])
            segs = [(TN, lambda kc, th=th: (("HID", kc, th), HID[:, kc * T + th * TN: kc * T + (th + 1) * TN]))
                    for th in range(NTH)]
            _dbg("gu%d" % hf)
            proj("wd%s_%d" % (tag, l), hf * D, DC, FCH, segs, evac_d, list(range(8)))
            _dbg("d%d" % hf)

    def conv_mixer(l):
        S.fence("BIG")
        for cc in range(DC if not unf else 0):
            S.emit("sp", lambda e, cc=cc: e.dma_start(out=halo_src[cc * 128:(cc + 1) * 128, :], in_=XT[:, cc, T - HALO:T]),
                   reads=[("XT", cc, NTH - 1)], writes=["halo_src"], dma="st_halo")
        if not unf:
            S.emit("pool", lambda e: e.collective_compute("AllGather", ALU.bypass, replica_groups=groups_b,
                                                          ins=[halo_src], outs=[halo_dst]),
                   reads=["halo_src"], writes=["halo_dst"], dma="ag_halo_%d" % l, inc=1)
        for r in range(c.NB):
            for cc in range(DC):
                S.emit("sp", lambda e, r=r, cc=cc: e.dma_start(
                    out=XHR[:, r, cc * HALO:(cc + 1) * HALO], in_=halo_dst[r * D + cc * 128: r * D + (cc + 1) * 128, :]),
                    reads=["halo_dst"], writes=["XHR"] + ALLHT, dma="ld_halo")
        XHf = XH[:].rearrange("p c h -> p (c h)")
        S.emit("dve", lambda e: e.tensor_scalar(out=XHf, in0=XHR[:, 0, :], scalar1=CV[:, c.cv_prevsel:c.cv_prevsel + 1],
                                                 scalar2=None, op0=ALU.mult),
               reads=["XHR", "CV"] + ALLHT, writes=["XH"])
        for r in range(1, c.NB):
            S.emit("dve", lambda e, r=r: e.scalar_tensor_tensor(
                out=XHf, in0=XHR[:, r, :], scalar=CV[:, c.cv_prevsel + r:c.cv_prevsel + r + 1], in1=XHf,
                op0=ALU.mult, op1=ALU.add), reads=["XHR", "CV", "XH"] + ALLHT, writes=["XH"])
        _dbg("halo")
        halo_seg = (HALO, lambda cc: ("XH", XH[:, cc, :]), lambda cc: ("HHT", HHT[:, cc, :]))
        import os
        v_ = os.environ.get("MKV", "0")
        if v_ == "0":
            rmsnorm([xseg(th) for th in range(NTH)] + [halo_seg], "nm_%d" % l, [0, 1, 2])
        elif v_ == "1":
            rmsnorm([xseg(th) for th in range(NTH)] + [halo_seg], "nm_%d" % l, [0, 1])
        elif v_ == "2":
            rmsnorm([xseg(th) for th in range(NTH)], "nm_%d" % l, [0, 1])
        elif v_ == "3":
            rmsnorm([halo_seg], "nm_%d" % l, [0, 1])
        elif v_ == "4":
            rmsnorm([halo_seg] + [xseg(th) for th in range(NTH)], "nm_%d" % l, [0, 1, 2])
        _dbg("cnorm")
        VV = BIG[:, 0:2 * DC * TN].bitcast(F32)
        YY = BIG[:, 2 * DC * TN: 3 * DC * TN]
        UE = BIG[:, 3 * DC * TN: 3 * DC * TN + 2 * UW].bitcast(F32)
        for th in range(NTH):
            def evac_pw1(oc, outs, th=th):
                if oc % 2 == 0:
                    evac_pw1.a = outs
                    return
                cc = oc // 2
                ub = 0
                ue = UE[:, ub * UW:(ub + 1) * UW]
                ures = ("UE", ub)
                for si, (lo, n) in enumerate([(0, HALO), (HALO, TN)]):
                    (ar, aa), (gr, ga) = evac_pw1.a[si], outs[si]
                    tr, ta = tf()
                    S.emit("act", lambda e, ta=ta, ga=ga, n=n, cc=cc: e.activation(
                        out=ta[:, 0:n], in_=ga, func=AF.Sigmoid, bias=pvc("pw1b_%d" % l, 2 * cc + 1)),
                        reads=[gr, "PV"], writes=[tr])
                    S.emit("dve", lambda e, ue=ue, lo=lo, n=n, aa=aa, ta=ta, cc=cc: e.scalar_tensor_tensor(
                        out=ue[:, lo:lo + n], in0=aa, scalar=pvc("pw1b_%d" % l, 2 * cc), in1=ta[:, 0:n],
                        op0=ALU.add, op1=ALU.mult), reads=[ar, tr, "PV"], writes=[ures])
                if th == 0:
                    S.emit("dve", lambda e, ue=ue: e.tensor_scalar(
                        out=ue[:, 0:HALO], in0=ue[:, 0:HALO], scalar1=CV[:, c.cv_hasprev:c.cv_hasprev + 1],
                        scalar2=None, op0=ALU.mult), reads=[ures, "CV"], writes=[ures])
                vo = VV[:, cc * TN:(cc + 1) * TN]
                vres = ("VV", cc)
                wcol = lambda k, cc=cc: pvc("dww_%d" % l, k * DC + cc)
                HN = TN // 2
                for hh_ in range(2):
                    S.emit("dve", lambda e, vo=vo, ue=ue, cc=cc, hh_=hh_: e.tensor_scalar(
                        out=vo[:, hh_ * HN:(hh_ + 1) * HN], in0=ue[:, 2 + hh_ * HN:2 + (hh_ + 1) * HN], scalar1=wcol(0),
                        scalar2=pvc("dwb_%d" % l, cc), op0=ALU.mult, op1=ALU.add), reads=[ures, "PV"], writes=[(vres, hh_)])
                for k in range(1, CONV_W):
                    for hh_ in range(2):
                        S.emit("dve", lambda e, vo=vo, ue=ue, k=k, hh_=hh_: e.scalar_tensor_tensor(
                            out=vo[:, hh_ * HN:(hh_ + 1) * HN], in0=ue[:, 2 + k + hh_ * HN:2 + k + (hh_ + 1) * HN], scalar=wcol(k),
                            in1=vo[:, hh_ * HN:(hh_ + 1) * HN], op0=ALU.mult, op1=ALU.add),
                            reads=[ures, (vres, hh_), "PV"], writes=[(vres, hh_)])
                S.emit("dve", lambda e: e.memset(S.fence_scratch, 0.0), reads=[(vres, 0), (vres, 1)], writes=[vres, "FSCR"])
            if th == 0:
                hs = (HALO, lambda kc: ("HHT", HHT[:, kc, :]))
            else:
                hs = (HALO, lambda kc, th=th: (("HT", kc, th - 1), HT[:, kc, th * TN - HALO: th * TN]))
            proj("wpw1_%d" % l, 0, 2 * DC, DC, [hs, hseg(th)], evac_pw1, [0, 1, 2, 3])
            _dbg("pw1_%d" % th)
            pm = PS[4][:, 0:TN]
            for cc in range(DC):
                S.emit("pe", lambda e, cc=cc: e.matmul(pm, lhsT=ONESD[:], rhs=VV[:, cc * TN:(cc + 1) * TN],
                                                        start=(cc == 0), stop=(cc == DC - 1)),
                       reads=[("VV", cc), "ONES"], writes=[("PS", 4)])
            mr, ma = tf()
            S.emit("act", lambda e, ma=ma: e.activation(out=ma, in_=pm, func=AF.Copy), reads=[("PS", 4)], writes=[mr])
            for cc in range(DC):
                vo = VV[:, cc * TN:(cc + 1) * TN]
                S.emit("dve", lambda e, vo=vo, ma=ma: e.tensor_tensor(out=vo, in0=vo, in1=ma, op=ALU.subtract),
                       reads=[("VV", cc), mr], writes=[("VV", cc)])
            pv_ = PS[5][:, 0:TN]
            for cc in range(DC):
                tr, ta = tf()
                S.emit("act", lambda e, ta=ta, cc=cc: e.activation(out=ta, in_=VV[:, cc * TN:(cc + 1) * TN], func=AF.Square),
                       reads=[("VV", cc)], writes=[tr])
                S.emit("pe", lambda e, ta=ta, cc=cc: e.matmul(pv_, lhsT=ONESD[:], rhs=ta, start=(cc == 0), stop=(cc == DC - 1)),
                       reads=[tr, "ONES"], writes=[("PS", 5)])
            sr, sa = tf()
            S.emit("act", lambda e, sa=sa: e.activation(out=sa, in_=pv_, func=AF.Sqrt, bias=EPST[:]),
                   reads=[("PS", 5), "EPST"], writes=[sr])
            rr, ra = tf()
            S.emit("dve", lambda e, ra=ra, sa=sa: e.reciprocal(out=ra, in_=sa), reads=[sr], writes=[rr])
            for cc in range(DC):
                vo = VV[:, cc * TN:(cc + 1) * TN]
                S.emit("dve", lambda e, vo=vo, ra=ra, cc=cc: e.scalar_tensor_tensor(
                    out=vo, in0=vo, scalar=pvc("lng_%d" % l, cc), in1=ra, op0=ALU.mult, op1=ALU.mult),
                    reads=[("VV", cc), rr, "PV"], writes=[("VV", cc)])
                S.emit("act", lambda e, vo=vo, cc=cc: e.activation(
                    out=YY[:, cc * TN:(cc + 1) * TN], in_=vo, func=AF.Silu, bias=pvc("lnb_%d" % l, cc)),
                    reads=[("VV", cc), "PV"], writes=[("YY", cc)])

            _dbg("ln_%d" % th)

            def evac_pw2(oc, outs, th=th):
                pr, pa = outs[0]
                xa = XT[:, oc, th * TN:(th + 1) * TN]
                S.emit("dve", lambda e, xa=xa, pa=pa, oc=oc: e.scalar_tensor_tensor(
                    out=xa, in0=pa, scalar=pvc("pw2b_%d" % l, oc), in1=xa, op0=ALU.add, op1=ALU.add),
                    reads=[pr, ("XT", oc, th), "PV"], writes=[("XT", oc, th)])
            proj("wpw2_%d" % l, 0, DC, DC, [(TN, lambda kc: (("YY", kc), YY[:, kc * TN:(kc + 1) * TN]))],
                 evac_pw2, [6, 7])
            _dbg("pw2_%d" % th)

    gvec_ready = [False]

    def build_gvec():
        S.fence("SCR")
        TABF = SMALL[0:33, 0:H]
        S.emit("dve", lambda e: e.memset(SMALL[0:64, 0:H], NEG), writes=["TAB"])
        S.emit("sp", lambda e: e.dma_start(out=SMALL[0:NBUCKET, 0:H], in_=relb_d), writes=["TAB"], dma="ld_tab")
        OHS = SCR[0:33, 0:2 * c.GW].bitcast(F32)
        S.emit("sp", lambda e: e.dma_start(out=OHS, in_=oh_d), writes=["OHS"], dma="ld_oh")
        GS = SCR[0:H, 2 * c.GW:3 * c.GW]
        for j0 in range(0, c.GW, 512):
            n = min(512, c.GW - j0)
            S.emit("pe", lambda e, j0=j0, n=n: e.matmul(PS[0][0:H, 0:n], lhsT=TABF, rhs=OHS[:, j0:j0 + n], start=True, stop=True),
                   reads=["TAB", "OHS"], writes=[("PS", 0)])
            S.emit("act", lambda e, j0=j0, n=n: e.activation(out=GS[:, j0:j0 + n], in_=PS[0][0:H, 0:n], func=AF.Copy),
                   reads=[("PS", 0)], writes=["GS"])
        for i in range(128):
            S.emit("sp", lambda e, i=i: e.dma_start(out=g2_d[:, i, :], in_=GS[:, 127 - i:127 - i + W2]),
                   reads=["GS"], writes=["gvec"], dma="st_gvec")
        S.fence("SCR")

    def attn_mixer(l, part="both"):
        NBLK, NBC, NQT, NKT = c.NBLK, c.NBC, c.NQT, c.NKT
        if part != "B":
            rmsnorm([xseg(th) for th in range(NTH)], "nm_%d" % l, [0, 1])
        QT = BIG[:, 0:H * T]
        MT = BIG[0:NBLK, H * T: H * T + 2 * T]
        KNS = BIG[:, H * T + 2 * T: H * T + 4 * T]
        VS = BIG[:, H * T + 4 * T: H * T + 6 * T]
        KM = SMALL[:, 32:32 + H * NBC]
        GQ = SMALL[:, 16:17]
        GK = 4
        BTW = T + (GK - 1) * 128
        BTb = BIG[:, H * T + 6 * T: H * T + 6 * T + 2 * BTW]
        S.fence("BIG")
        if part == "B":
            S.emit("sp", lambda e: e.dma_start(out=QT, in_=qt_in),
                   writes=[("QT", h_, th_) for h_ in range(H) for th_ in range(NTH)], dma="ld_qt")
        S.emit("dve", lambda e: e.tensor_scalar(out=GQ, in0=pvc("qn_%d" % l, 0), scalar1=float(128 ** -0.5), scalar2=None,
                                                 op0=ALU.mult), reads=["PV"], writes=["GQ"])

        def evac_qkv(oc, outs):
            h, kind = oc // 3, oc % 3
            for th in range(NTH):
                pr, pa = outs[th]
                if kind == 2:
                    vr, va = tb()
                    S.emit("act", lambda e, va=va, pa=pa: e.activation(out=va, in_=pa, func=AF.Copy), reads=[pr], writes=[vr])
                    for tt in range(TN // 128):
                        qt = th * (TN // 128) + tt
                        b = 4 + (qt % 2)
                        ptb = PS[b][:].bitcast(BF16)[:, 0:128]
                        S.emit("pe", lambda e, ptb=ptb, va=va, tt=tt: e.transpose(ptb, va[:, tt * 128:(tt + 1) * 128], IDB[:]),
                               reads=[vr, "IDB"], writes=[("PS", b)])
                        vs = VS[:, (h % 2) * T + qt * 128:(h % 2) * T + (qt + 1) * 128]
                        S.emit("dve", lambda e, vs=vs, ptb=ptb: e.tensor_copy(out=vs, in_=ptb),
                               reads=[("PS", b)], writes=[("VS", h % 2)])
                    continue
                rr_, raw = tf()
                S.emit("act", lambda e, raw=raw, pa=pa: e.activation(out=raw, in_=pa, func=AF.Copy), reads=[pr], writes=[rr_])
                sr, sq = tf()
                S.emit("act", lambda e, sq=sq, pa=pa: e.activation(out=sq, in_=pa, func=AF.Square), reads=[pr], writes=[sr])
                b = 6 + (th % 2)
                S.emit("pe", lambda e, sq=sq, b=b: e.matmul(PS[b][:, 0:TN], lhsT=ONESH[:], rhs=sq, start=True, stop=True),
                       reads=[sr, "ONES"], writes=[("PS", b)])
                dr, sd = tf()
                S.emit("act", lambda e, sd=sd, b=b: e.activation(out=sd, in_=PS[b][:, 0:TN], func=AF.Sqrt, bias=EPST[:]),
                       reads=[("PS", b), "EPST"], writes=[dr])
                S.emit("dve", lambda e, sd=sd: e.reciprocal(out=sd, in_=sd), reads=[dr], writes=[dr])
                if kind == 0:
                    o = QT[:, h * T + th * TN: h * T + (th + 1) * TN]
                    S.emit("dve", lambda e, o=o, raw=raw, sd=sd: e.scalar_tensor_tensor(
                        out=o, in0=raw, scalar=GQ, in1=sd, op0=ALU.mult, op1=ALU.mult),
                        reads=[rr_, dr, "GQ"], writes=[("QT", h, th)])
                else:
                    o = KNS[:, (h % 2) * T + th * TN:(h % 2) * T + (th + 1) * TN]
                    S.emit("dve", lambda e, o=o, raw=raw, sd=sd: e.scalar_tensor_tensor(
                        out=o, in0=raw, scalar=pvc("kn_%d" % l, 0), in1=sd, op0=ALU.mult, op1=ALU.mult),
                        reads=[rr_, dr, "PV"], writes=[("KNS", h % 2)])
            if kind == 1:
                ks = KNS[:, (h % 2) * T:(h % 2 + 1) * T]
                S.emit("dve", lambda e, ks=ks, h=h: e.tensor_reduce(
                    out=KM[:, h * NBC:(h + 1) * NBC], in_=ks.rearrange("p (b k) -> p b k", k=BLK), axis=AX.X, op=ALU.add),
                    reads=[("KNS", h % 2)], writes=["KM"])
                S.emit("sp", lambda e, ks=ks, h=h: e.dma_start(out=kx_src[h * 128:(h + 1) * 128, :], in_=ks),
                       reads=[("KNS", h % 2)], writes=[("kx_src", h % 2)], dma="st_k%d" % (h % 2))
            if kind == 2:
                vs = VS[:, (h % 2) * T:(h % 2 + 1) * T]
                S.emit("sp", lambda e, vs=vs, h=h: e.dma_start(out=vx_src[h * 128:(h + 1) * 128, :], in_=vs),
                       reads=[("VS", h % 2)], writes=[("vx_src", h % 2)], dma="st_v%d" % (h % 2))
        _dbg("anorm")
        if part != "B":
            proj("wqkv_%d" % l, 0, 3 * H, DC, [hseg(th) for th in range(NTH)], evac_qkv, [0, 1, 2, 3], defer=1)
            _dbg("qkv")
            S.emit("dve", lambda e: e.tensor_scalar(out=KM, in0=KM, scalar1=1.0 / BLK, scalar2=None, op0=ALU.mult),
                   reads=["KM"], writes=["KM"])
            S.emit("sp", lambda e: e.dma_start(out=km_src, in_=KM), reads=["KM"], writes=["km_src"], dma="st_km")
        if part == "A":
            S.emit("sp", lambda e: e.dma_start(out=qt_out, in_=QT),
                   reads=[("QT", h_, th_) for h_ in range(H) for th_ in range(NTH)], writes=["qt_out"], dma="st_qt")
            return
        if part == "both":
            for nm, src, dst in (("km", km_src, km_dst), ("kx", kx_src, kx_dst), ("vx", vx_src, vx_dst)):
                S.emit("pool", lambda e, src=src, dst=dst: e.collective_compute(
                    "AllGather", ALU.bypass, replica_groups=groups_b, ins=[src], outs=[dst]),
                    reads=[nm + "_src", (nm + "_src", 0), (nm + "_src", 1)], writes=[nm + "_dst"], dma="ag_%s_%d" % (nm, l), inc=1)
        S.fence("SCR")
        KMA = SCR[:, 0:2 * H * NBLK].bitcast(F32)
        KMB = SCR[:, 2 * H * NBLK: 3 * H * NBLK]
        for r in range(c.NB):
            dst = KMA.rearrange("p (h n) -> p h n", n=NBLK)[:, :, r * NBC:(r + 1) * NBC]
            S.emit("sp", lambda e, r=r, dst=dst: e.dma_start(
                out=dst, in_=km_dst[r * 128:(r + 1) * 128, :].rearrange("p (h n) -> p h n", n=NBC)),
                reads=["km_dst"], writes=["KMA"], dma="ld_kma")
        S.emit("dve", lambda e: e.tensor_copy(out=KMB, in_=KMA), reads=["KMA"], writes=["KMB"])
        _dbg("exch")
        o0 = 3 * H * NBLK
        o0 = (o0 + 63) // 64 * 64
        KTA = [SCR[:, o0 + i * c.S: o0 + (i + 1) * c.S] for i in range(2)]
        VTA = [SCR[:, o0 + (2 + i) * c.S: o0 + (3 + i) * c.S] for i in range(2)]
        BT = [BTb[:, i * BTW:(i + 1) * BTW] for i in range(2)]
        assert o0 + 4 * c.S <= SCRN, (o0 + 4 * c.S, SCRN)
        assert H * T + 6 * T + 2 * BTW <= BIGN, (H * T + 6 * T + 2 * BTW, BIGN)
        assert NKT % GK == 0
        bt_i = [0]
        for h in range(H):
            hb = h % 2
            for r in range(c.NB):
                S.emit("sp", lambda e, r=r, h=h, hb=hb: e.dma_start(
                    out=KTA[hb][:, r * T:(r + 1) * T], in_=kx_dst[(r * H + h) * 128:(r * H + h + 1) * 128, :]),
                    reads=["kx_dst"], writes=[("KTA", hb)], dma="ld_kta%d" % hb)
                S.emit("sp", lambda e, r=r, h=h, hb=hb: e.dma_start(
                    out=VTA[hb][:, r * T:(r + 1) * T], in_=vx_dst[(r * H + h) * 128:(r * H + h + 1) * 128, :]),
                    reads=["vx_dst"], writes=[("VTA", hb)], dma="ld_vta%d" % hb)
            mt = MT[:, hb * T:(hb + 1) * T]
            for qt in range(NQT):
                th, tq = qt // (TN // 128), qt % (TN // 128)
                qa = QT[:, h * T + qt * 128: h * T + (qt + 1) * 128]
                b = 6 + (qt % 2)
                S.emit("pe", lambda e, qa=qa, h=h, b=b: e.matmul(PS[b][:, 0:NBLK], lhsT=qa, rhs=KMB[:, h * NBLK:(h + 1) * NBLK],
                                                                 start=True, stop=True),
                       reads=[("QT", h, th), "KMB"], writes=[("PS", b)])
                gr, gt_ = tf()
                gm = gt_[:, 0:NBLK]
                m8 = gt_[:, 32:40]
                sel = gt_[:, 64:64 + NBLK]
                S.emit("dve", lambda e, gm=gm, b=b, qt=qt: e.tensor_tensor(
                    out=gm, in0=PS[b][:, 0:NBLK], in1=CV[:, c.cv_pastneg + qt * NBLK: c.cv_pastneg + (qt + 1) * NBLK], op=ALU.add),
                    reads=[("PS", b), "CV"], writes=[gr])
                S.emit("dve", lambda e, gm=gm, m8=m8: e.max(out=m8, in_=gm), reads=[gr], writes=[gr])
                S.emit("dve", lambda e, gm=gm, m8=m8, sel=sel: e.tensor_scalar(
                    out=sel, in0=gm, scalar1=m8[:, 2:3], scalar2=None, op0=ALU.is_ge), reads=[gr], writes=[gr])
                S.emit("dve", lambda e, sel=sel, qt=qt: e.tensor_tensor(
                    out=sel, in0=sel, in1=CV[:, c.cv_ownhot + qt * NBLK: c.cv_ownhot + (qt + 1) * NBLK], op=ALU.max),
                    reads=[gr, "CV"], writes=[gr])
                mr_, mb = tb()
                S.emit("dve", lambda e, sel=sel, mb=mb: e.tensor_scalar(
                    out=mb[:, 0:NBLK], in0=sel, scalar1=-1.0, scalar2=-NEG, op0=ALU.add, op1=ALU.mult),
                    reads=[gr], writes=[mr_])
                b2 = 4 + (qt % 2)
                ptb = PS[b2][:].bitcast(BF16)[0:NBLK, 0:128]
                S.emit("pe", lambda e, ptb=ptb, mb=mb: e.transpose(ptb, mb[:, 0:NBLK], IDB[:]),
                       reads=[mr_, "IDB"], writes=[("PS", b2)])
                S.emit("act", lambda e, ptb=ptb, mt=mt, qt=qt: e.activation(out=mt[:, qt * 128:(qt + 1) * 128], in_=ptb, func=AF.Copy),
                       reads=[("PS", b2)], writes=[("MT", hb)])
            _dbg("gate%d" % h)
            tiles = [(kt, qc) for kt in range(NKT) for qc in range(NTH)]
            SBANK = [0, 1, 6]
            bt_of, pts = {}, {}

            def stage1(i, h=h, hb=hb, mt=mt):
                kt, qc = tiles[i]
                if qc == 0 and kt % GK == 0:
                    bi = bt_i[0] % 2
                    bt_i[0] += 1
                    bt_of[kt // GK] = bi
                    m0 = c.S - 128 - (kt + GK - 1) * 128
                    src = g2_d[h, :, m0:m0 + BTW]
                    S.emit("sp", lambda e, bi=bi, src=src: e.dma_start(out=BT[bi], in_=src),
                           reads=["gvec"], writes=[("BT", bi)], dma="ld_bt%d" % bi)
                bi = bt_of[kt // GK]
                bo = ((kt // GK) * GK + GK - 1 - kt) * 128
                nblk = (kt * 128) // BLK
                sb_ = SBANK[i % 3]
                ps_s = PS[sb_][:, 0:TN]
                qa = QT[:, h * T + qc * TN: h * T + (qc + 1) * TN]
                S.emit("pe", lambda e: e.matmul(
                    ps_s, lhsT=KTA[hb][:, kt * 128:(kt + 1) * 128], rhs=qa, start=True, stop=False),
                    reads=[("KTA", hb), ("QT", h, qc)], writes=[("PS", sb_)])
                S.emit("pe", lambda e: e.matmul(
                    ps_s, lhsT=ESB[:, nblk * 128:(nblk + 1) * 128], rhs=mt[:, qc * TN:(qc + 1) * TN], start=False, stop=True),
                    reads=[("MT", hb), "ESB"], writes=[("PS", sb_)])
                tr, ta = tf()
                S.emit("dve", lambda e: e.tensor_tensor(
                    out=ta, in0=ps_s, in1=BT[bi][:, bo + qc * TN: bo + (qc + 1) * TN], op=ALU.add),
                    reads=[("PS", sb_), ("BT", bi)], writes=[tr])
                pr_, pt = tb()
                S.emit("act", lambda e: e.activation(out=pt, in_=ta, func=AF.Exp), reads=[tr], writes=[pr_])
                pts[i] = (pr_, pt)

            def stage2(i, hb=hb):
                kt, qc = tiles[i]
                pr_, pt = pts.pop(i)
                S.emit("pe", lambda e: e.matmul(
                    PS[2 + qc][:, 0:TN], lhsT=VTA[hb][:, kt * 128:(kt + 1) * 128], rhs=pt, start=(kt == 0), stop=(kt == NKT - 1)),
                    reads=[("VTA", hb), pr_], writes=[("PS", 2 + qc)])
                S.emit("pe", lambda e: e.matmul(
                    PS[4 + qc][:, 0:TN], lhsT=ONESB[:], rhs=pt, start=(kt == 0), stop=(kt == NKT - 1)),
                    reads=[pr_, "ONES"], writes=[("PS", 4 + qc)])
            DEPTH = 2
            for i in range(len(tiles) + DEPTH):
                if i < len(tiles):
                    stage1(i)
                if i - DEPTH >= 0:
                    stage2(i - DEPTH)
            _dbg("core%d" % h)
            for qc in range(NTH):
                rr, ra = tf()
                S.emit("dve", lambda e, ra=ra, qc=qc: e.reciprocal(out=ra, in_=PS[4 + qc][:, 0:TN]),
                       reads=[("PS", 4 + qc)], writes=[rr])
                S.emit("dve", lambda e, ra=ra, qc=qc, h=h: e.tensor_tensor(
                    out=HT[:, h, qc * TN:(qc + 1) * TN], in0=PS[2 + qc][:, 0:TN], in1=ra, op=ALU.mult),
                    reads=[("PS", 2 + qc), rr], writes=[("HT", h, qc)])

        _dbg("heads")
        S.fence("SCR")

        def evac_o(oc, outs):
            for th in range(NTH):
                pr, pa = outs[th]
                xa = XT[:, oc, th * TN:(th + 1) * TN]
                S.emit("dve", lambda e, xa=xa, pa=pa: e.tensor_tensor(out=xa, in0=xa, in1=pa, op=ALU.add),
                       reads=[pr, ("XT", oc, th)], writes=[("XT", oc, th)])
        proj("wo_%d" % l, 0, DC, DC, [hseg(th) for th in range(NTH)], evac_o, [0, 1, 6, 7])

    phases = []
    if unf:
        phases = list(steps)
        if "attnB" in kinds:
            build_gvec()
    else:
        for l in range(L):
            phases.append(("ffn1", l))
            phases.append(("mix", l))
            phases.append(("ffn2", l))
        if L > 1:
            build_gvec()
    try:
      for kind, l in phases:
        if kind == "ffn1":
            ffn("1", l)
        elif kind == "ffn2":
            ffn("2", l)
        elif kind == "conv":
            conv_mixer(l)
        elif kind == "attnA":
            attn_mixer(l, "A")
        elif kind == "attnB":
            attn_mixer(l, "B")
        elif l % 2 == 0:
            conv_mixer(l)
        else:
            attn_mixer(l)
        if stop_after == (kind, l):
            break
    except _Stop:
        pass

    for cc in range(DC):
        S.emit("sp", lambda e, cc=cc: e.dma_start(out=yT_d[cc * 128:(cc + 1) * 128, :], in_=XT[:, cc, :]),
               reads=[("XT", cc, th) for th in range(NTH)], writes=["yT"], dma="st_y")
    with nc.Block() as block:
        S.finalize(block)
    es.close()
    return nc


def run(cfg, inputs, ag_weights=True, n_layers=None, stop_after=None, trace=False):
    shared, per_core = host_prepare(cfg, inputs)
    nc = build_program(cfg, ag_weights=ag_weights, n_layers=n_layers, stop_after=stop_after)
    L = cfg.L if n_layers is None else n_layers
    wnames = [s[0] for s in cfg.weight_specs() if int(s[0].split("_")[1]) < L]
    in_maps = []
    for core in range(NCORES):
        m = dict(per_core[core])
        for k in ("pvec", "relb", "ident", "esel"):
            m[k] = shared[k]
        for name in wnames:
            w = shared[name]
            if ag_weights:
                r = w.shape[0] // NCORES
                m[name] = w[core * r:(core + 1) * r]
            else:
                m[name] = w
        in_maps.append(m)
    res = run_bass_kernel_spmd(nc, in_maps, core_ids=list(range(NCORES)), trace=trace)
    B = NCORES // cfg.NB
    out = np.zeros((B, cfg.S, cfg.D), np.float32)
    for core in range(NCORES):
        b, cl = core // cfg.NB, core % cfg.NB
        out[b, cl * cfg.T:(cl + 1) * cfg.T, :] = res.results[core]["yT"].T
    return out, res


LAUNCHES = [
    [("ffn1", 0)],
    [("conv", 0), ("ffn2", 0), ("ffn1", 1), ("attnA", 1)],
    [("attnB", 1), ("ffn2", 1), ("ffn1", 2)],
    [("conv", 2), ("ffn2", 2), ("ffn1", 3), ("attnA", 3)],
    [("attnB", 3), ("ffn2", 3)],
]


def run_unfused(cfg, inputs, launches=None):
    c = cfg
    launches = LAUNCHES if launches is None else launches
    shared, per_core = host_prepare(cfg, inputs)
    xT = [per_core[i]["xT"] for i in range(NCORES)]
    carry = None
    for steps in launches:
        nc = build_program(cfg, steps=steps)
        kinds = set(k for k, _ in steps)
        need = set()
        for kind, l in steps:
            need |= {"ffn1": {"wgu1_%d" % l, "wd1_%d" % l}, "ffn2": {"wgu2_%d" % l, "wd2_%d" % l},
                     "conv": {"wpw1_%d" % l, "wpw2_%d" % l}, "attnA": {"wqkv_%d" % l}, "attnB": {"wo_%d" % l}}[kind]
        in_maps = []
        for core in range(NCORES):
            b = core // c.NB
            ranks = list(range(b * c.NB, (b + 1) * c.NB))
            m = {"xT": xT[core], "cvec": per_core[core]["cvec"], "oh": per_core[core]["oh"]}
            for k in ("pvec", "relb", "ident", "esel"):
                m[k] = shared[k]
            for name in need:
                m[name] = shared[name]
            if "conv" in kinds:
                m["halo_dst"] = np.concatenate([xT[r][:, c.T - HALO:] for r in ranks], axis=0)
            if "attnB" in kinds:
                m["qt_in"] = carry[core]["qt_out"]
                for nm in ("kx", "vx", "km"):
                    m[nm + "_dst"] = np.concatenate([carry[r][nm + "_src"] for r in ranks], axis=0)
            in_maps.append(m)
        res = run_bass_kernel_spmd(nc, in_maps, core_ids=list(range(NCORES)))
        xT = [res.results[i]["yT"] for i in range(NCORES)]
        carry = res.results
    B = NCORES // c.NB
    out = np.zeros((B, c.S, c.D), np.float32)
    for core in range(NCORES):
        b, cl = core // c.NB, core % c.NB
        out[b, cl * c.T:(cl + 1) * c.T, :] = xT[core].T
    return out


def kernel(**inputs):
    cfg = Cfg(**FULL_CFG)
    return run_unfused(cfg, inputs)
```
